# Optimizing a Trainium2 kernel written in Bass

```python
import jax, jax.numpy as jnp
from jax import lax
import numpy as np

D_MODEL = 1024
BATCH = 2
SEQ = 16384
DEPTH = 2
DEC_BATCH = 16
DEC_SEQ = 64
PAST_LEN = 2048

CHUNK = 64
EPS = 1e-6
D_MIX = D_MODEL
D_CONV = D_MIX // 2
CONV_W = 3
D_MLSTM = D_MIX - D_CONV
MLSTM_HEADS = 4
MLSTM_HD = D_MLSTM // MLSTM_HEADS
N_MEM = 256
MEM_HEADS = 4
MEM_HD = D_MODEL // MEM_HEADS
D_FF = 2816
FF_CONV_W = 3
D_IN = 3 * D_CONV + 4 * D_MLSTM + 2 * MLSTM_HEADS
SPLITS = (D_CONV, 2 * D_CONV, 3 * D_CONV, 3 * D_CONV + D_MLSTM, 3 * D_CONV + 2 * D_MLSTM,
          3 * D_CONV + 3 * D_MLSTM, 3 * D_CONV + 4 * D_MLSTM)

kernel_name = 'hymba_conv_mlstm_streaming_step'


def rms_norm(x, g):
    xf = x.astype(jnp.float32)
    y = xf * lax.rsqrt(jnp.mean(xf * xf, axis=-1, keepdims=True) + EPS)
    return (y * g.astype(jnp.float32)).astype(x.dtype)


def causal_dwconv(u, buf, w):
    T = u.shape[1]
    ext = jnp.concatenate([buf.astype(u.dtype), u], axis=1)
    y = ext[:, 0:T] * w[0]
    for j in range(1, w.shape[0]):
        y = y + ext[:, j:j + T] * w[j]
    return y, ext[:, T:]


def mlstm_chunkwise(q, k, v, i_pre, logf, C0, n0, m0):
    B, H, T, DH = q.shape
    L = min(CHUNK, T)
    NC = T // L

    def chunks(a):
        return jnp.moveaxis(a.reshape(a.shape[:2] + (NC, L) + a.shape[3:]), 2, 0)

    causal = jnp.tril(jnp.ones((L, L), dtype=bool))

    def step(carry, blk):
        C, n, m = carry
        qb, kb, vb, ib, fb = blk
        bcum = jnp.cumsum(fb, axis=-1)
        d_log = bcum[..., :, None] - bcum[..., None, :] + ib[..., None, :]
        d_log = jnp.where(causal, d_log, -jnp.inf)
        inter_log = bcum + m[..., None]
        m_t = jnp.maximum(jnp.max(d_log, axis=-1), inter_log)
        s = jnp.einsum('bhtd,bhsd->bhts', qb, kb) * jnp.exp(d_log - m_t[..., None])
        w_inter = jnp.exp(inter_log - m_t)
        num = jnp.einsum('bhts,bhse->bhte', s, vb) + w_inter[..., None] * jnp.einsum('bhtd,bhde->bhte', qb, C)
        den = jnp.sum(s, axis=-1) + w_inter * jnp.einsum('bhtd,bhd->bht', qb, n)
        h = num / jnp.maximum(jnp.abs(den), jnp.exp(-m_t))[..., None]
        m_new = m_t[..., -1]
        w_s = jnp.exp(bcum[..., -1:] - bcum + ib - m_new[..., None])
        w_c = jnp.exp(bcum[..., -1] + m - m_new)
        C_new = w_c[..., None, None] * C + jnp.einsum('bhs,bhsd,bhse->bhde', w_s, kb, vb)
        n_new = w_c[..., None] * n + jnp.einsum('bhs,bhsd->bhd', w_s, kb)
        return (C_new, n_new, m_new), h

    (C, n, m), hs = lax.scan(step, (C0, n0, m0),
                             (chunks(q), chunks(k), chunks(v), chunks(i_pre), chunks(logf)))
    h = jnp.moveaxis(hs, 0, 2).reshape(B, H, T, DH)
    return h, C, n, m


def token_mixer(xn, conv_buf, C0, n0, m0, w_in, b_gates, conv_w, mlstm_norm_w, w_out):
    B, T, _ = xn.shape
    f32 = jnp.float32
    xc, g_b, g_c, q, k, v, o, gates = jnp.split(xn @ w_in, SPLITS, axis=-1)
    y_a, conv_new = causal_dwconv(g_c * xc, conv_buf, conv_w)
    y_a = g_b * y_a
    def heads(a):
        return a.astype(f32).reshape(B, T, MLSTM_HEADS, MLSTM_HD).transpose(0, 2, 1, 3)
    gates = (gates + b_gates).astype(f32)
    i_pre = gates[..., :MLSTM_HEADS].transpose(0, 2, 1)
    logf = jax.nn.log_sigmoid(gates[..., MLSTM_HEADS:]).transpose(0, 2, 1)
    h, C, n, m = mlstm_chunkwise(heads(q), heads(k) * (MLSTM_HD ** -0.5), heads(v), i_pre, logf, C0, n0, m0)
    h = h * lax.rsqrt(jnp.mean(h * h, axis=-1, keepdims=True) + EPS)
    h = h.transpose(0, 2, 1, 3).reshape(B, T, D_MLSTM) * mlstm_norm_w.astype(f32)
    y_b = (jax.nn.sigmoid(o.astype(f32)) * h).astype(xn.dtype)
    y = jnp.concatenate([y_a, y_b], axis=-1) @ w_out
    return y, conv_new, C, n, m


def memory_kv(mem, g, w_k, w_v):
    B = mem.shape[0]
    mn = rms_norm(mem, g)
    mk = (mn @ w_k).reshape(B, N_MEM, MEM_HEADS, MEM_HD)
    mv = (mn @ w_v).reshape(B, N_MEM, MEM_HEADS, MEM_HD)
    return mk, mv


def memory_attend(xn, mk, mv, w_q, w_o):
    B, T, _ = xn.shape
    q = (xn @ w_q).reshape(B, T, MEM_HEADS, MEM_HD)
    s = jnp.einsum('bthd,bmhd->bhtm', q, mk).astype(jnp.float32) * (MEM_HD ** -0.5)
    p = jax.nn.softmax(s, axis=-1).astype(xn.dtype)
    o = jnp.einsum('bhtm,bmhd->bthd', p, mv).reshape(B, T, D_MODEL)
    return o @ w_o


def conv_ffn(xn, ff_buf, w_up, ffn_conv_w, w_down):
    up, ff_new = causal_dwconv(xn @ w_up, ff_buf, ffn_conv_w)
    a, g = jnp.split(up, 2, axis=-1)
    return (jax.nn.gelu(g) * a) @ w_down, ff_new


def trunk_layer(x, mk, mv, conv_buf, C0, n0, m0, ff_buf, p):
    y, conv_new, C, n, m = token_mixer(rms_norm(x, p['norm_mix_pre']), conv_buf, C0, n0, m0,
                                       p['w_in'], p['b_gates'], p['conv_w'], p['mlstm_norm_w'], p['w_out'])
    x = x + rms_norm(y, p['norm_mix_post'])
    y = memory_attend(rms_norm(x, p['norm_mem_pre']), mk, mv, p['w_mq'], p['w_mo'])
    x = x + rms_norm(y, p['norm_mem_post'])
    y, ff_new = conv_ffn(rms_norm(x, p['norm_ffn_pre']), ff_buf, p['w_up'], p['ffn_conv_w'], p['w_down'])
    x = x + rms_norm(y, p['norm_ffn_post'])
    return x, conv_new, C, n, m, ff_new


def setup_inputs(seed: int = 0) -> dict:
    key = jax.random.key(seed)
    ks = jax.random.split(key, 40)
    f32 = jnp.float32

    def nrm(k, shape, s):
        return jax.random.normal(k, shape, f32) * s

    def gain(k, shape):
        return 1.0 + 0.05 * jax.random.normal(k, shape, f32)

    b_i = 0.1 * jax.random.normal(ks[0], (DEPTH, MLSTM_HEADS), f32)
    b_f = jnp.linspace(3.0, 6.0, MLSTM_HEADS, dtype=f32)[None, :] + 0.1 * jax.random.normal(ks[1], (DEPTH, MLSTM_HEADS), f32)
    return {
        'x_prompt': nrm(ks[2], (BATCH, SEQ, D_MODEL), 1.0),
        'x_sample': nrm(ks[3], (DEC_BATCH, DEC_SEQ, D_MODEL), 1.0),
        'mem_prompt': nrm(ks[4], (BATCH, N_MEM, D_MODEL), 1.0),
        'cache_mem_k': nrm(ks[5], (DEPTH, DEC_BATCH, N_MEM, MEM_HEADS, MEM_HD), 1.0),
        'cache_mem_v': nrm(ks[6], (DEPTH, DEC_BATCH, N_MEM, MEM_HEADS, MEM_HD), 1.0),
        'state_conv': nrm(ks[7], (DEPTH, DEC_BATCH, CONV_W - 1, D_CONV), 1.0),
        'state_mlstm_C': nrm(ks[8], (DEPTH, DEC_BATCH, MLSTM_HEADS, MLSTM_HD, MLSTM_HD), 0.1),
        'state_mlstm_n': nrm(ks[9], (DEPTH, DEC_BATCH, MLSTM_HEADS, MLSTM_HD), 0.5),
        'state_mlstm_m': nrm(ks[10], (DEPTH, DEC_BATCH, MLSTM_HEADS), 0.5),
        'state_ffn_conv': nrm(ks[11], (DEPTH, DEC_BATCH, FF_CONV_W - 1, 2 * D_FF), 1.0),
        'norm_mix_pre': gain(ks[12], (DEPTH, D_MODEL)),
        'w_in': nrm(ks[13], (DEPTH, D_MODEL, D_IN), D_MODEL ** -0.5),
        'b_gates': jnp.concatenate([b_i, b_f], axis=-1),
        'conv_w': nrm(ks[14], (DEPTH, CONV_W, D_CONV), CONV_W ** -0.5),
        'mlstm_norm_w': gain(ks[15], (DEPTH, D_MLSTM)),
        'w_out': nrm(ks[16], (DEPTH, D_MIX, D_MODEL), D_MIX ** -0.5),
        'norm_mix_post': gain(ks[17], (DEPTH, D_MODEL)),
        'norm_mem_pre': gain(ks[18], (DEPTH, D_MODEL)),
        'norm_mem_kv': gain(ks[19], (DEPTH, D_MODEL)),
        'w_mq': nrm(ks[20], (DEPTH, D_MODEL, D_MODEL), D_MODEL ** -0.5),
        'w_mk': nrm(ks[21], (DEPTH, D_MODEL, D_MODEL), D_MODEL ** -0.5),
        'w_mv': nrm(ks[22], (DEPTH, D_MODEL, D_MODEL), D_MODEL ** -0.5),
        'w_mo': nrm(ks[23], (DEPTH, D_MODEL, D_MODEL), D_MODEL ** -0.5),
        'norm_mem_post': gain(ks[24], (DEPTH, D_MODEL)),
        'norm_ffn_pre': gain(ks[25], (DEPTH, D_MODEL)),
        'w_up': nrm(ks[26], (DEPTH, D_MODEL, 2 * D_FF), D_MODEL ** -0.5),
        'ffn_conv_w': nrm(ks[27], (DEPTH, FF_CONV_W, 2 * D_FF), FF_CONV_W ** -0.5),
        'w_down': nrm(ks[28], (DEPTH, D_FF, D_MODEL), D_FF ** -0.5),
        'norm_ffn_post': gain(ks[29], (DEPTH, D_MODEL)),
    }


def reference(x_prompt, x_sample, mem_prompt, cache_mem_k, cache_mem_v, state_conv, state_mlstm_C,
              state_mlstm_n, state_mlstm_m, state_ffn_conv,
              norm_mix_pre, w_in, b_gates, conv_w, mlstm_norm_w, w_out, norm_mix_post,
              norm_mem_pre, norm_mem_kv, w_mq, w_mk, w_mv, w_mo, norm_mem_post,
              norm_ffn_pre, w_up, ffn_conv_w, w_down, norm_ffn_post):
    f32 = jnp.float32
    xp, xs = x_prompt, x_sample
    B = xp.shape[0]
    mk_p_all, mv_p_all = [], []
    conv_p_all, conv_s_all = [], []
    C_p_all, C_s_all, n_p_all, n_s_all, m_p_all, m_s_all = [], [], [], [], [], []
    ff_p_all, ff_s_all = [], []
    for l in range(DEPTH):
        p = {'norm_mix_pre': norm_mix_pre[l], 'w_in': w_in[l], 'b_gates': b_gates[l], 'conv_w': conv_w[l],
             'mlstm_norm_w': mlstm_norm_w[l], 'w_out': w_out[l], 'norm_mix_post': norm_mix_post[l],
             'norm_mem_pre': norm_mem_pre[l], 'w_mq': w_mq[l], 'w_mo': w_mo[l], 'norm_mem_post': norm_mem_post[l],
             'norm_ffn_pre': norm_ffn_pre[l], 'w_up': w_up[l], 'ffn_conv_w': ffn_conv_w[l], 'w_down': w_down[l],
             'norm_ffn_post': norm_ffn_post[l]}
        mk_p, mv_p = memory_kv(mem_prompt, norm_mem_kv[l], w_mk[l], w_mv[l])
        xp, conv_p, C_p, n_p, m_p, ff_p = trunk_layer(
            xp, mk_p, mv_p,
            jnp.zeros((B, CONV_W - 1, D_CONV), xp.dtype),
            jnp.zeros((B, MLSTM_HEADS, MLSTM_HD, MLSTM_HD), f32),
            jnp.zeros((B, MLSTM_HEADS, MLSTM_HD), f32),
            jnp.zeros((B, MLSTM_HEADS), f32),
            jnp.zeros((B, FF_CONV_W - 1, 2 * D_FF), xp.dtype), p)
        xs, conv_s, C_s, n_s, m_s, ff_s = trunk_layer(
            xs, cache_mem_k[l], cache_mem_v[l], state_conv[l],
            state_mlstm_C[l].astype(f32), state_mlstm_n[l].astype(f32), state_mlstm_m[l].astype(f32),
            state_ffn_conv[l], p)
        mk_p_all.append(mk_p); mv_p_all.append(mv_p)
        conv_p_all.append(conv_p); conv_s_all.append(conv_s)
        C_p_all.append(C_p); C_s_all.append(C_s)
        n_p_all.append(n_p); n_s_all.append(n_s)
        m_p_all.append(m_p); m_s_all.append(m_s)
        ff_p_all.append(ff_p); ff_s_all.append(ff_s)
    st = jnp.stack
    return (xp, xs, st(mk_p_all), st(mv_p_all), st(conv_p_all), st(conv_s_all),
            st(C_p_all), st(C_s_all), st(n_p_all), st(n_s_all), st(m_p_all), st(m_s_all),
            st(ff_p_all), st(ff_s_all))
```

```python
import numpy as np
from contextlib import ExitStack
import concourse.bass as bass
import concourse.mybir as mybir
from concourse.bass_utils import run_bass_kernel_spmd

F32 = mybir.dt.float32
BF16 = mybir.dt.bfloat16
AF = mybir.ActivationFunctionType
ALU = mybir.AluOpType
AX = mybir.AxisListType

DEPTH = 2
D = 1024
KC = 8
DFF = 2816
NJ = 22
NMEM = 256
EPS = 1e-6
NCORES = 8
PSEQ = 16384
NTP = 512
NSLOT = 4
NU = 34
U_XC, U_B, U_C, U_Q, U_K, U_V, U_O = 0, 1, 2, 3, 4, 5, 6
U_OUT, U_MQ, U_MO, U_UP, U_DN, U_MK, U_MV = 7, 9, 11, 13, 24, 30, 32
DN_GROUPS = ((0, 8), (8, 8), (16, 6))
CV_MIXPRE, CV_MEMPRE, CV_MEMKV, CV_FFNPRE, CV_CONVW, CV_NW, CV_FCW = 0, 8, 16, 24, 32, 44, 48


class _Op:
    __slots__ = ("eng", "fn", "r", "w", "is_dma", "dkey", "dval", "signal", "sigcount",
                 "waits", "idx")


class Sched:
    ENGS = ("pe", "act", "dve", "pool", "sp")

    def __init__(self):
        self.ops = []
        self.eops = {e: [] for e in self.ENGS}
        self.last_w = {}
        self.readers = {}
        self.dma_cnt = {}
        self.dma_last = {}
        self.tags = {}
        self.phase = ""
        self.inst_tag = {}

    def _add(self, op):
        op.idx = len(self.ops)
        self.tags[op.idx] = self.phase
        op.signal = False
        op.sigcount = None
        op.waits = []
        deps = []
        psr = tuple(k for k in op.r if isinstance(k, tuple) and k and k[0] == "PS")
        if psr:
            op.r = tuple(k for k in op.r if k not in psr)
            op.w = tuple(op.w) + psr
        for k in op.r:
            p = self.last_w.get(k)
            if p is not None:
                deps.append((p, "raw"))
        for k in op.w:
            p = self.last_w.get(k)
            if p is not None:
                deps.append((p, "waw"))
            for q in self.readers.get(k, ()):
                deps.append((q, "war"))
        if op.is_dma:
            p = self.dma_last.get(op.dkey)
            if p is not None:
                deps.append((p, "raw"))
        latest = {}
        for p, kind in deps:
            if p is not op and not p.is_dma:
                q = latest.get(p.eng)
                if q is None or p.idx > q.idx:
                    latest[p.eng] = p
        deps = [(p, kind) for p, kind in deps if p.is_dma or latest.get(p.eng) is p]
        seen = set()
        for p, kind in deps:
            if p is op:
                continue
            need = True
            if (not p.is_dma) and (not op.is_dma) and p.eng == op.eng:
                need = op.eng != "pe"
            if need and p.idx not in seen:
                seen.add(p.idx)
                op.waits.append(p)
                if not p.is_dma:
                    p.signal = True
        for k in op.r:
            self.readers.setdefault(k, []).append(op)
        for k in op.w:
            self.last_w[k] = op
            self.readers[k] = []
        if op.is_dma:
            self.dma_cnt[op.dkey] = self.dma_cnt.get(op.dkey, 0) + 16
            op.dval = self.dma_cnt[op.dkey]
            self.dma_last[op.dkey] = op
        self.ops.append(op)
        self.eops[op.eng].append(op)
        return op

    def op(self, eng, fn, r=(), w=()):
        o = _Op()
        o.eng, o.fn, o.r, o.w = eng, fn, tuple(r), tuple(w)
        o.is_dma, o.dkey, o.dval = False, None, None
        return self._add(o)

    def dma(self, eng, fn, key, r=(), w=()):
        o = _Op()
        o.eng, o.fn, o.r, o.w = eng, fn, tuple(r), tuple(w)
        o.is_dma, o.dkey, o.dval = True, key, None
        return self._add(o)

    def emit(self, nc):
        for e in self.ENGS:
            c = 0
            for o in self.eops[e]:
                if o.signal:
                    c += 1
                    o.sigcount = c
        with ExitStack() as st:
            esem = {e: st.enter_context(nc.semaphore("s_" + e)) for e in self.ENGS}
            dsem = {}
            for i, k in enumerate(self.dma_cnt):
                dsem[k] = st.enter_context(nc.semaphore("d%d" % i))
            block = st.enter_context(nc.Block())

            def run(e, engobj):
                waited = {}
                for o in self.eops[e]:
                    need = {}
                    for p in o.waits:
                        if p.is_dma:
                            sem, val, sk = dsem[p.dkey], p.dval, ("d", p.dkey)
                        else:
                            sem, val, sk = esem[p.eng], p.sigcount, ("e", p.eng)
                        if sk not in need or need[sk][1] < val:
                            need[sk] = (sem, val)
                    for sk, (sem, val) in need.items():
                        if waited.get(sk, 0) < val:
                            engobj.wait_ge(sem, val)
                            waited[sk] = val
                    inst = o.fn(engobj)
                    try:
                        self.inst_tag[inst.ins.name] = self.tags[o.idx]
                    except Exception:
                        pass
                    if o.is_dma:
                        inst.then_inc(dsem[o.dkey], 16)
                    elif o.signal:
                        inst.then_inc(esem[e], 1)
                if e == "sp":
                    for k, v in self.dma_cnt.items():
                        engobj.wait_ge(dsem[k], v)

            block.tensor(lambda t: run("pe", t))
            block.scalar(lambda t: run("act", t))
            block.vector(lambda t: run("dve", t))
            block.gpsimd(lambda t: run("pool", t))
            block.sync(lambda t: run("sp", t))


class Builder:
    def __init__(self, tp):
        self.TP = tp
        self.nc = bass.Bass("TRN2", target_bir_lowering=False)
        self.S = Sched()
        self.st = ExitStack()
        self.rot = 0
        self.ring_seq = 0
        self.cv_rr = 0

    def mm(self, out, lhsT, rhs, start, stop, r, w):
        self.S.op("pe", lambda e: e.matmul(out, lhsT=lhsT, rhs=rhs, start=start, stop=stop), r, w)

    def tr(self, out, in_, ident, r, w):
        self.S.op("pe", lambda e: e.transpose(out, in_=in_, identity=ident), r, w)

    def act(self, out, in_, func, r, w, **kw):
        self.S.op("act", lambda e: e.activation(out=out, in_=in_, func=func, **kw), r, w)

    def tt(self, eng, out, in0, in1, op, r, w):
        self.S.op(eng, lambda e: e.tensor_tensor(out=out, in0=in0, in1=in1, op=op), r, w)

    def ts(self, eng, out, in0, s1, s2, op0, op1, r, w):
        if op1 is None:
            self.S.op(eng, lambda e: e.tensor_scalar(out=out, in0=in0, scalar1=s1, scalar2=None, op0=op0), r, w)
        else:
            self.S.op(eng, lambda e: e.tensor_scalar(out=out, in0=in0, scalar1=s1, scalar2=s2, op0=op0, op1=op1), r, w)

    def stt(self, out, in0, scalar, in1, op0, op1, r, w):
        self.S.op("dve", lambda e: e.scalar_tensor_tensor(out=out, in0=in0, scalar=scalar, in1=in1, op0=op0, op1=op1), r, w)

    def cp(self, eng, out, in_, r, w):
        self.S.op(eng, lambda e: e.tensor_copy(out=out, in_=in_), r, w)

    def memset(self, eng, ap, val, w):
        self.S.op(eng, lambda e: e.memset(ap, val), (), w)

    def recip(self, out, in_, r, w):
        self.S.op("dve", lambda e: e.reciprocal(out=out, in_=in_), r, w)

    def dma(self, eng, out, in_, key, r, w):
        self.S.dma(eng, lambda e: e.dma_start(out=out, in_=in_), key, r, w)

    def sb(self, name, shape, dt):
        return self.st.enter_context(self.nc.sbuf_tensor(name, shape, dt))

    def bank(self):
        b = self.rot % 3
        self.rot += 1
        return self.ps[b], ("PS", b)

    def declare(self):
        nc = self.nc
        I = lambda n, s: nc.dram_tensor(n, s, F32, kind="ExternalInput").ap()
        O = lambda n, s: nc.dram_tensor(n, s, F32, kind="ExternalOutput").ap()
        TP = max(self.TP, NTP)
        d = {}
        d["xs"] = I("xs", [2, 64, D])
        d["cmk"] = I("cmk", [DEPTH, 2, NMEM, D])
        d["cmv"] = I("cmv", [DEPTH, 2, NMEM, D])
        d["sconv"] = I("sconv", [DEPTH, 2, 2, 512])
        d["sC"] = I("sC", [DEPTH, 2, 4, 128, 128])
        d["sn"] = I("sn", [DEPTH, 2, 4, 128])
        d["sm"] = I("sm", [DEPTH, 2, 4])
        d["sffn"] = I("sffn", [DEPTH, 2, 2, 2 * DFF])
        d["xp"] = I("xp", [TP, D])
        d["mem"] = I("mem", [NMEM, D])
        for n in ("norm_mix_pre", "norm_mix_post", "norm_mem_pre", "norm_mem_kv", "norm_mem_post",
                  "norm_ffn_pre", "norm_ffn_post"):
            d[n] = I(n, [DEPTH, D])
        d["w_in"] = I("w_in", [DEPTH, D, 3592])
        d["b_gates"] = I("b_gates", [DEPTH, 8])
        d["conv_w"] = I("conv_w", [DEPTH, 3, 512])
        d["mlstm_norm_w"] = I("mlstm_norm_w", [DEPTH, 512])
        for n in ("w_out", "w_mq", "w_mk", "w_mv", "w_mo"):
            d[n] = I(n, [DEPTH, D, D])
        d["w_up"] = I("w_up", [DEPTH, D, 2 * DFF])
        d["ffn_conv_w"] = I("ffn_conv_w", [DEPTH, 3, 2 * DFF])
        d["w_down"] = I("w_down", [DEPTH, DFF, D])
        d["ys"] = O("ys", [2, 64, D])
        d["o_conv_s"] = O("o_conv_s", [DEPTH, 2, 2, 512])
        d["o_C_s"] = O("o_C_s", [DEPTH, 2, 4, 128, 128])
        d["o_n_s"] = O("o_n_s", [DEPTH, 2, 4, 128])
        d["o_m_s"] = O("o_m_s", [DEPTH, 2, 4])
        d["o_ffn_s"] = O("o_ffn_s", [DEPTH, 2, 2, 2 * DFF])
        d["yp"] = O("yp", [TP, D])
        d["o_mk_p"] = O("o_mk_p", [DEPTH, NMEM, D])
        d["o_mv_p"] = O("o_mv_p", [DEPTH, NMEM, D])
        d["o_conv_p"] = O("o_conv_p", [DEPTH, 1, 2, 512])
        d["o_C_p"] = O("o_C_p", [DEPTH, 1, 4, 128, 128])
        d["o_n_p"] = O("o_n_p", [DEPTH, 1, 4, 128])
        d["o_m_p"] = O("o_m_p", [DEPTH, 1, 4])
        d["o_ffn_p"] = O("o_ffn_p", [DEPTH, 1, 2, 2 * DFF])
        d["wscr"] = nc.dram_tensor("wscr", [DEPTH, NU, 128, KC * 512], BF16, kind="Internal").ap()
        self.d = d

    def alloc(self):
        sb = self.sb
        NS = NTP // 128
        self.x = sb("x", [128, NS, D], F32)
        self.xn = sb("xn", [128, NS, D], BF16)
        self.xnT = sb("xnT", [128, KC, NTP], BF16)
        self.junk = sb("junk", [128, D], BF16)
        self.junk2 = sb("junk2", [128, D], BF16)
        self.ybuf = sb("ybuf", [128, NS, D], F32)
        self.G1 = sb("G1", [128, 4, 516], F32)
        self.G2 = sb("G2", [128, 4, 516], F32)
        self.so = sb("so", [128, 4, NTP], BF16)
        self.ubuf = sb("ubuf", [128, 4, 516], F32)
        self.GB1 = sb("GB1", [128, KC, NTP], BF16)
        self.GB2 = sb("GB2", [128, KC, NTP], BF16)
        self.ktm = sb("ktm", [128, NS, 512], BF16)
        self.v1 = sb("v1", [128, NS, 4, 130], BF16)
        self.gat = sb("gat", [128, NS, 8], F32)
        self.hT = sb("hT", [128, NJ, NTP], BF16)
        self.ring = [sb("ring%d" % i, [128, KC, 512], BF16) for i in range(NSLOT)]
        self.mkT = [sb("mkT%d" % l, [128, KC, NMEM], BF16) for l in range(DEPTH)]
        self.mv = [sb("mv%d" % l, [128, 2, D], BF16) for l in range(DEPTH)]
        self.grep = sb("grep", [128, D], F32)
        self.colv = sb("colv", [128, DEPTH, 256], F32)
        self.brep = sb("brep", [128, DEPTH, 8], F32)
        self.wg = sb("wg", [128, DEPTH, KC, 8], BF16)
        self.ident = sb("ident", [128, 128], F32)
        self.identb = sb("identb", [128, 128], BF16)
        self.ones = sb("ones", [128, 128], F32)
        self.onesb = sb("onesb", [128, 128], BF16)
        self.triu = sb("triu", [128, 128], F32)
        self.stA = sb("stA", [128, 128], F32)
        self.stB = sb("stB", [128, 128], F32)
        self.Cn = [sb("Cn%d" % l, [128, 4, 129], F32) for l in range(DEPTH)]
        self.mrep = [sb("mrep%d" % l, [128, 4], F32) for l in range(DEPTH)]
        self.uh = [sb("uh%d" % l, [128, 4, 2], F32) for l in range(DEPTH)]
        self.ffh = [sb("ffh%d" % l, [128, 44, 2], F32) for l in range(DEPTH)]
        self.nhalf = sb("nhalf", [128, 8], F32)
        self.tail = sb("tail", [128, 128], F32)
        self.tailr = sb("tailr", [128, 128], F32)
        self.tailr2 = sb("tailr2", [128, 128], F32)
        self.stat = sb("stat", [128, 64], F32)
        self.lgE = sb("lgE", [128, NS, 4], F32)
        self.lgF = sb("lgF", [128, NS, 4], F32)
        self.lgA = sb("lgA", [128, NS, 4], F32)
        self.clS = sb("clS", [128, NS, 4], F32)
        self.clLs = sb("clLs", [128, NS, 4], F32)
        self.amaxS = sb("amaxS", [128, NS, 4], F32)
        self.Rm = sb("Rm", [128, 4, 128], F32)
        self.ms = sb("ms", [128, NS, 32], F32)
        self.nms = sb("nms", [128, NS, 32], F32)
        self.amx = sb("amx", [16, 1], F32)
        self.amd = sb("amd", [16, 16], F32)
        self.Csb = sb("Csb", [128, 4, 130], BF16)
        self.kw = sb("kw", [128, 4, 128], BF16)
        self.smb = sb("smb", [128, 4, 128], BF16)
        self.hn = sb("hn", [128, 4, 128], F32)
        ps = self.st.enter_context
        nc = self.nc
        self.ps = [ps(nc.psum_tensor("psr%d" % i, [128, 512], F32)) for i in range(3)]
        self.pstr = ps(nc.psum_tensor("pstr", [128, 1024], BF16))
        self.psA = ps(nc.psum_tensor("psA", [128, 512], F32))
        self.pstr2 = self.psA[:, :].bitcast(BF16)
        self.psS = ps(nc.psum_tensor("psS", [128, 512], F32))
        self.psH = ps(nc.psum_tensor("psH", [128, 512], F32))
        self.psC = ps(nc.psum_tensor("psC", [128, 512], F32))

    def consts(self):
        d = self.d
        S = self.S
        self.memset("pool", self.ones[:], 1.0, ["ones"])
        self.memset("pool", self.ident[:], 1.0, ["ident"])
        S.op("pool", lambda e: e.affine_select(out=self.ident[:], in_=self.ident[:], pattern=[[-1, 128]],
                                               compare_op=ALU.is_equal, fill=0.0, base=0, channel_multiplier=1),
             ["ident"], ["ident"])
        self.memset("pool", self.triu[:], 1.0, ["triu"])
        S.op("pool", lambda e: e.affine_select(out=self.triu[:], in_=self.triu[:], pattern=[[1, 128]],
                                               compare_op=ALU.is_ge, fill=0.0, base=0, channel_multiplier=-1),
             ["triu"], ["triu"])
        self.cp("dve", self.identb[:], self.ident[:], ["ident"], ["identb"])
        self.cp("dve", self.onesb[:], self.ones[:], ["ones"], ["onesb"])
        self.memset("dve", self.v1[:], 1.0, ["v1all"])
        self.memset("dve", self.Csb[:], 0.0, ["Csb"])
        self.memset("dve", self.stB[:], 0.0, ["stB"])
        self.memset("dve", self.stA[:], 0.0, ["stA"])
        self.memset("dve", self.tail[:], 0.0, ["tail"])
        self.memset("dve", self.nhalf[:], -0.5, ["nhalf"])
        for l in range(DEPTH):
            rows = [(d["norm_mix_pre"][l].rearrange("(c p) -> c p", p=128), 8),
                    (d["norm_mem_pre"][l].rearrange("(c p) -> c p", p=128), 8),
                    (d["norm_mem_kv"][l].rearrange("(c p) -> c p", p=128), 8),
                    (d["norm_ffn_pre"][l].rearrange("(c p) -> c p", p=128), 8),
                    (d["conv_w"][l].rearrange("j (c p) -> (j c) p", p=128), 12),
                    (d["mlstm_norm_w"][l].rearrange("(c p) -> c p", p=128), 4)]
            r0 = 0
            for i, (ap, n) in enumerate(rows):
                self.dma("sp", self.stA[r0:r0 + n, :], ap, ("stA", i), [], ["stA"])
                r0 += n
            fcw = d["ffn_conv_w"][l].rearrange("j (c p) -> (j c) p", p=128)
            self.dma("sp", self.stA[48:128, :], fcw[0:80, :], ("stA", 6), [], ["stA"])
            self.dma("sp", self.stB[0:52, :], fcw[80:132, :], ("stB", 0), [], ["stB"])
            b, bk = self.bank()
            self.tr(b[:, 0:128], self.stA[:], self.ident[:], ["stA", "ident"], [bk])
            self.tr(b[:, 128:256], self.stB[:], self.ident[:], ["stB", "ident"], [bk])
            self.cp("dve", self.colv[:, l, :], b[:, 0:256], [bk], [("colv", l)])
            self.dma("sp", self.brep[:, l, :], d["b_gates"][l].partition_broadcast(128), ("brep", l), [], [("brep", l)])
            self.dma("pool", self.wg[:, l, :, :],
                     d["w_in"][l][:, 3584:3592].rearrange("(kc p) f -> p kc f", p=128),
                     ("wg", l), [], [("wg", l)])

    def cv(self, l, col):
        return self.colv[:, l, col:col + 1]

    def convert(self, l, only=None):
        d = self.d

        def scr(u):
            return d["wscr"][l, u].rearrange("p (kc f) -> p kc f", kc=KC)

        def cv(dst, src, u):
            key = ("cv", self.cv_rr % 6)
            self.cv_rr += 1
            self.dma("pool", dst, src, key, [], [("wscr", l, u)])

        def colunit(w, c0):
            return w[:, c0:c0 + 512].rearrange("(kc p) f -> p kc f", p=128)

        units = []
        for i in range(7):
            units.append((i, [(scr(i), colunit(d["w_in"][l], i * 512))]))
        for h in range(2):
            units.append((U_OUT + h, [(scr(U_OUT + h), colunit(d["w_out"][l], h * 512))]))
            units.append((U_MQ + h, [(scr(U_MQ + h), colunit(d["w_mq"][l], h * 512))]))
            units.append((U_MO + h, [(scr(U_MO + h), colunit(d["w_mo"][l], h * 512))]))
            units.append((U_MK + h, [(scr(U_MK + h), colunit(d["w_mk"][l], h * 512))]))
            units.append((U_MV + h, [(scr(U_MV + h), colunit(d["w_mv"][l], h * 512))]))
        for u in range(11):
            wu = d["w_up"][l]
            a = wu[:, 2 * u * 128:2 * u * 128 + 256].rearrange("(kc p) f -> p kc f", p=128)
            g = wu[:, DFF + 2 * u * 128:DFF + 2 * u * 128 + 256].rearrange("(kc p) f -> p kc f", p=128)
            units.append((U_UP + u, [(scr(U_UP + u)[:, :, 0:256], a), (scr(U_UP + u)[:, :, 256:512], g)]))
        for h in range(2):
            for gi, (j0, nj) in enumerate(DN_GROUPS):
                src = d["w_down"][l][j0 * 128:(j0 + nj) * 128, h * 512:(h + 1) * 512].rearrange("(j p) f -> p j f", p=128)
                units.append((U_DN + h * 3 + gi, [(scr(U_DN + h * 3 + gi)[:, 0:nj, :], src)]))
        units.sort(key=lambda t: t[0])
        for u, lst in units:
            if only is not None and u not in only:
                continue
            for dst, src in lst:
                cv(dst, src, u)

    def wload(self, l, u, nj=KC):
        slot = self.ring_seq % NSLOT
        self.ring_seq += 1
        t = self.ring[slot]
        src = self.d["wscr"][l, u].rearrange("p (kc f) -> p kc f", kc=KC)
        self.dma("sp", t[:, 0:nj, :], src[:, 0:nj, :], ("ring", slot), [("wscr", l, u)], [("ring", slot)])
        return t, ("ring", slot)

    def norm_T(self, l, srcs, nt, cvbase, dst, dkey, ncols=None):
        prevphase = self.S.phase
        self.S.phase = "norm"
        st = self.stat
        off = 0
        offs = []
        for s, (ap, key, P) in enumerate(srcs):
            if s % 2 == 0:
                self.act(self.junk[0:P, :], ap, AF.Square, [key], ["junk", ("st", s)], accum_out=st[0:P, s:s + 1])
            else:
                self.S.op("dve", lambda e, ap=ap, P=P, s=s: e.scalar_tensor_tensor(
                    out=self.junk2[0:P, :], in0=ap, scalar=1.0, in1=ap, op0=ALU.mult, op1=ALU.mult,
                    accum_out=st[0:P, s:s + 1]), [key], ["junk2", ("st", s)])
            self.ts("pool", st[0:P, 8 + s:9 + s], st[0:P, s:s + 1], 1.0 / D, EPS, ALU.mult, ALU.add,
                    [("st", s)], [("st", 8 + s)])
            self.tt("pool", st[0:P, 24 + s:25 + s], st[0:P, 8 + s:9 + s], self.nhalf[0:P, 0:1], ALU.pow,
                    [("st", 8 + s), "nhalf"], [("st", 24 + s)])
            self.ts("dve", self.xn[0:P, s, :], ap, st[0:P, 24 + s:25 + s], None, ALU.mult, None,
                    [key, ("st", 24 + s)], [("xn", s)])
            offs.append((off, P))
            off += P
        for c in range(KC):
            tb, tk = (self.pstr, ("PS", "tr")) if c % 2 == 0 else (self.pstr2, ("PS", "A"))
            for s, (o, P) in enumerate(offs):
                self.tr(tb[:, o:o + P], self.xn[0:P, s, c * 128:(c + 1) * 128],
                        self.identb[0:P, 0:P], [("xn", s), "identb"], [tk])
            self.act(dst[:, c, 0:nt], tb[:, 0:nt], AF.Copy, [tk, ("colv", l)],
                     [(dkey, c)], scale=self.cv(l, cvbase + c))
        self.S.phase = prevphase

    def load_grep(self, name, l):
        self.dma("sp", self.grep[:], self.d[name][l].partition_broadcast(128), "grep", [], ["grep"])

    def post_evac(self, b, bk, s, h, P):
        st = self.stat
        c = 32 + s * 2 + h
        self.act(self.junk[0:P, h * 512:(h + 1) * 512], b[0:P, :], AF.Square, [bk], [("junkh", h), ("st", c)],
                 accum_out=st[0:P, c:c + 1])
        self.tt("dve", self.ybuf[0:P, s, h * 512:(h + 1) * 512], b[0:P, :], self.grep[0:P, h * 512:(h + 1) * 512],
                ALU.mult, [bk, "grep"], [("ybuf", s)])

    def post_chain(self, nsub, P):
        st = self.stat
        for s in range(nsub):
            c = 32 + s * 2
            self.tt("pool", st[0:P, 40 + s:41 + s], st[0:P, c:c + 1], st[0:P, c + 1:c + 2], ALU.add,
                    [("st", c), ("st", c + 1)], [("st", 40 + s)])
            self.ts("pool", st[0:P, 44 + s:45 + s], st[0:P, 40 + s:41 + s], 1.0 / D, EPS, ALU.mult, ALU.add,
                    [("st", 40 + s)], [("st", 44 + s)])
            self.tt("pool", st[0:P, 52 + s:53 + s], st[0:P, 44 + s:45 + s], self.nhalf[0:P, 0:1], ALU.pow,
                    [("st", 44 + s), "nhalf"], [("st", 52 + s)])
            self.stt(self.x[0:P, s, :], self.ybuf[0:P, s, :], st[0:P, 52 + s:53 + s], self.x[0:P, s, :],
                     ALU.mult, ALU.add, [("ybuf", s), ("st", 52 + s), ("x", s)], [("x", s)])

    def proj_post(self, lhsT_fn, lkey_fn, unit_fn, nsub, P):
        self.S.phase = "post"
        for h in range(2):
            units = unit_fn(h)
            steps = []
            for (wt, wk, j0, nj) in units:
                for j in range(nj):
                    steps.append((wt, wk, j0 + j, j))
            for s in range(nsub):
                b, bk = self.bank()
                for i, (wt, wk, jg, jl) in enumerate(steps):
                    self.mm(b[0:P, :], lhsT_fn(jg, s), wt[:, jl, :], i == 0, i == len(steps) - 1,
                            [lkey_fn(jg), wk], [bk])
                self.post_evac(b, bk, s, h, P)
        self.post_chain(nsub, P)

    def proj_down(self, l, nsub, P):
        self.S.phase = "post"
        hb = [(self.psA, ("PS", "A")), (self.psS, ("PS", "S")), (self.psH, ("PS", "H")), (self.psC, ("PS", "C"))]
        ng = len(DN_GROUPS)
        for h in range(2):
            for gi, (j0, nj) in enumerate(DN_GROUPS):
                wt, wk = self.wload(l, U_DN + h * 3 + gi, nj)
                for s in range(nsub):
                    B, Bk = hb[s]
                    for jl in range(nj):
                        j = j0 + jl
                        self.mm(B[0:P, :], self.hT[:, j, s * 128:s * 128 + P], wt[:, jl, :],
                                gi == 0 and jl == 0, gi == ng - 1 and jl == nj - 1, [("hT", j), wk], [Bk])
            for s in range(nsub):
                B, Bk = hb[s]
                self.post_evac(B, Bk, s, h, P)
        self.post_chain(nsub, P)

    def layer(self, l, nt, L):
        stage = 99
        if stage < 1:
            return
        nsub = max(nt // 128, 1)
        P = min(nt, 128)
        nch = nt // L
        qT = lambda h: self.GB1[:, h, :]
        kT = lambda h: self.GB1[:, 4 + h, :]
        catT = self.GB2
        xcb, Bb = self.G1, self.G2
        xsrc = [(self.x[0:P, s, :], ("x", s), P) for s in range(nsub)]
        self.load_grep("norm_mix_post", l)
        self.norm_T(l, xsrc, nt, CV_MIXPRE, self.xnT, "xnT")
        if stage < 2:
            return
        self.cp("pool", self.ubuf[:, :, 0:2], self.uh[l][:, :, :], [("uh", l)], [("ubufh",)] + [("ubuf", c) for c in range(4)])
        self.S.phase = "win"

        def fm_unit(u, evac):
            wt, wk = self.wload(l, u)
            for cc in range(4):
                b, bk = self.bank()
                for kc in range(KC):
                    self.mm(b[:, 0:nt], wt[:, kc, cc * 128:(cc + 1) * 128], self.xnT[:, kc, 0:nt], kc == 0, kc == KC - 1,
                            [wk, ("xnT", kc)], [bk])
                evac(cc, b, bk)
            return wt, wk

        fm_unit(U_XC, lambda cc, b, bk: self.act(xcb[:, cc, 0:nt], b[:, 0:nt], AF.Copy, [bk], [("G1", cc)]))
        fm_unit(U_B, lambda cc, b, bk: self.act(Bb[:, cc, 0:nt], b[:, 0:nt], AF.Copy, [bk], [("G2", cc)]))
        fm_unit(U_C, lambda cc, b, bk: self.tt("dve", self.ubuf[:, cc, 2:2 + nt], b[:, 0:nt], xcb[:, cc, 0:nt], ALU.mult,
                                                [bk, ("G1", cc)], [("ubuf", cc)]))
        fm_unit(U_Q, lambda cc, b, bk: self.act(qT(cc)[:, 0:nt], b[:, 0:nt], AF.Copy, [bk], [("GB1", cc)]))
        wtk, wkk = fm_unit(U_K, lambda cc, b, bk: self.act(kT(cc)[:, 0:nt], b[:, 0:nt], AF.Copy, [bk], [("GB1", 4 + cc)],
                                                          scale=128.0 ** -0.5))
        for ci in range(nch):
            off = ci * L
            b, bk = self.bank()
            for kc in range(KC):
                self.mm(b[0:L, :], self.xnT[:, kc, off:off + L], wtk[:, kc, :], kc == 0, kc == KC - 1, [("xnT", kc), wkk], [bk])
            self.act(self.ktm[0:L, ci, :], b[0:L, :], AF.Copy, [bk], [("ktm", ci)], scale=128.0 ** -0.5)
        wtv, wkv = self.wload(l, U_V)
        for ci in range(nch):
            off = ci * L
            b, bk = self.bank()
            for kc in range(KC):
                self.mm(b[0:L, :], self.xnT[:, kc, off:off + L], wtv[:, kc, :], kc == 0, kc == KC - 1, [("xnT", kc), wkv], [bk])
            self.act(self.v1[0:L, ci, :, 0:128], b[0:L, :].rearrange("p (h e) -> p h e", h=4), AF.Copy, [bk, "v1all"], [("v1", ci)])
            for kc in range(KC):
                self.mm(self.psH[0:L, 408:416], self.xnT[:, kc, off:off + L], self.wg[:, l, kc, :], kc == 0, kc == KC - 1,
                        [("xnT", kc), ("wg", l)], [("PS", "H")])
            self.tt("dve", self.gat[0:L, ci, :], self.psH[0:L, 408:416], self.brep[0:L, l, :], ALU.add,
                    [("PS", "H"), ("brep", l)], [("gat", ci)])
        self.S.phase = "mlstm"
        self.mlstm_pre(l, nch, L)
        self.S.phase = "win"
        fm_unit(U_O, lambda cc, b, bk: self.act(self.so[:, cc, 0:nt], b[:, 0:nt], AF.Sigmoid, [bk], [("so", cc)]))
        if stage < 3:
            return
        self.S.phase = "conva"
        for c in range(4):
            acc = self.ybuf[:, c % 2, 0:nt]
            ak = ("ybuf", c % 2)
            uk = [("ubuf", c), ("ubufh",), ("colv", l)]
            self.act(acc, self.ubuf[:, c, 0:nt], AF.Copy, uk, [ak], scale=self.cv(l, CV_CONVW + 0 * 4 + c))
            self.stt(acc, self.ubuf[:, c, 1:1 + nt], self.cv(l, CV_CONVW + 1 * 4 + c), acc, ALU.mult, ALU.add, uk + [ak], [ak])
            self.stt(acc, self.ubuf[:, c, 2:2 + nt], self.cv(l, CV_CONVW + 2 * 4 + c), acc, ALU.mult, ALU.add, uk + [ak], [ak])
            self.tt("dve", catT[:, c, 0:nt], acc, Bb[:, c, 0:nt], ALU.mult, [ak, ("G2", c)], [("GB2", c)])
        self.cp("pool", self.uh[l][:, :, :], self.ubuf[:, :, nt:nt + 2], [("ubuf", c) for c in range(4)] + [("ubufh",)], [("uh", l)])
        if stage < 4:
            return
        self.S.phase = "mlstm"
        for ci in range(nch):
            self.mlstm_chunk(l, ci, ci * L, L)
        self.mlstm_tail(l, nch, L)
        if stage < 5:
            return

        def out_units(u0):
            def f(h):
                wt, wk = self.wload(l, u0 + h)
                return [(wt, wk, 0, KC)]
            return f

        self.proj_post(lambda j, s: catT[:, j, s * 128:s * 128 + P], lambda j: ("GB2", j), out_units(U_OUT), nsub, P)
        if stage < 6:
            return
        self.load_grep("norm_mem_post", l)
        self.norm_T(l, xsrc, nt, CV_MEMPRE, self.xnT, "xnT")
        self.S.phase = "attn"
        qmT = self.GB1
        for h2 in range(2):
            wt, wk = self.wload(l, U_MQ + h2)
            for cc in range(4):
                c = h2 * 4 + cc
                b, bk = self.bank()
                for kc in range(KC):
                    self.mm(b[:, 0:nt], wt[:, kc, cc * 128:(cc + 1) * 128], self.xnT[:, kc, 0:nt], kc == 0, kc == KC - 1,
                            [wk, ("xnT", kc)], [bk])
                self.act(qmT[:, c, 0:nt], b[:, 0:nt], AF.Copy, [bk], [("GB1", c)], scale=1.0 / 16.0)
        pT = self.xn[:].rearrange("p a b -> p (a b)").rearrange("p (c t) -> p c t", t=NTP)
        oT = self.GB2
        rden = self.G1
        for h in range(4):
            for mc in range(2):
                b, bk = self.bank()
                for dc in range(2):
                    self.mm(b[:, 0:nt], self.mkT[l][:, h * 2 + dc, mc * 128:(mc + 1) * 128], qmT[:, h * 2 + dc, 0:nt],
                            dc == 0, dc == 1, [("mkT", l), ("GB1", h * 2 + dc)], [bk])
                self.act(pT[:, h * 2 + mc, 0:nt], b[:, 0:nt], AF.Exp, [bk], [("xn", h)])
            b, bk = self.bank()
            for mc in range(2):
                self.mm(b[:, 0:nt], self.onesb[:, :], pT[:, h * 2 + mc, 0:nt], mc == 0, mc == 1,
                        ["onesb", ("xn", h)], [bk])
            self.act(rden[:, h, 0:nt], b[:, 0:nt], AF.Ln, [bk], [("G1", h)])
            self.act(rden[:, h, 0:nt], rden[:, h, 0:nt], AF.Exp, [("G1", h)], [("G1", h)], scale=-1.0)
            for ec in range(2):
                b, bk = self.bank()
                for mc in range(2):
                    self.mm(b[:, 0:nt], self.mv[l][:, mc, h * 256 + ec * 128:h * 256 + (ec + 1) * 128],
                            pT[:, h * 2 + mc, 0:nt], mc == 0, mc == 1, [("mv", l), ("xn", h)], [bk])
                self.tt("dve", oT[:, h * 2 + ec, 0:nt], b[:, 0:nt], rden[:, h, 0:nt], ALU.mult,
                        [bk, ("G1", h)], [("GB2", h * 2 + ec)])
        self.proj_post(lambda j, s: oT[:, j, s * 128:s * 128 + P], lambda j: ("GB2", j), out_units(U_MO), nsub, P)
        if stage < 7:
            return
        self.load_grep("norm_ffn_post", l)
        self.norm_T(l, xsrc, nt, CV_FFNPRE, self.xnT, "xnT")
        self.S.phase = "ffnup"
        pend = None
        for u in range(11):
            wt, wk = self.wload(l, U_UP + u)
            for pj in range(2):
                j = 2 * u + pj
                res = []
                for part in range(2):
                    ch = j if part == 0 else NJ + j
                    b, bk = self.bank()
                    col = part * 256 + pj * 128
                    for kc in range(KC):
                        self.mm(b[:, 0:nt], wt[:, kc, col:col + 128], self.xnT[:, kc, 0:nt], kc == 0, kc == KC - 1,
                                [wk, ("xnT", kc)], [bk])
                    GG, gn = (self.G1, "G1") if j % 2 == 0 else (self.G2, "G2")
                    fa = GG[:, part * 2 + 0, :]
                    acc = GG[:, part * 2 + 1, 0:nt]
                    fk, ak = (gn, part * 2), (gn, part * 2 + 1)
                    fhk = ("ffh", l, ch)
                    self.cp("pool", fa[:, 0:2], self.ffh[l][:, ch, :], [fhk], [fk])
                    self.act(fa[:, 2:2 + nt], b[:, 0:nt], AF.Copy, [bk], [fk])
                    cw = lambda tap, ch=ch: self.cv(l, CV_FCW + tap * 44 + ch)
                    self.act(acc, b[:, 0:nt], AF.Copy, [bk, ("colv", l)], [ak], scale=cw(2))
                    self.stt(acc, fa[:, 1:1 + nt], cw(1), acc, ALU.mult, ALU.add, [fk, ak, ("colv", l)], [ak])
                    self.stt(acc, fa[:, 0:nt], cw(0), acc, ALU.mult, ALU.add, [fk, ak, ("colv", l)], [ak])
                    self.cp("pool", self.ffh[l][:, ch, :], fa[:, nt:nt + 2], [fk], [fhk])
                    res.append((acc, ak))
                if pend is not None:
                    self.act(pend[2], pend[2], AF.Gelu_apprx_tanh, [pend[3]], [pend[3]])
                    self.tt("dve", self.hT[:, pend[4], 0:nt], pend[2], pend[0], ALU.mult, [pend[3], pend[1]], [("hT", pend[4])])
                (acca, aka), (accg, akg) = res
                pend = (acca, aka, accg, akg, j)

        if pend is not None:
            self.act(pend[2], pend[2], AF.Gelu_apprx_tanh, [pend[3]], [pend[3]])
            self.tt("dve", self.hT[:, pend[4], 0:nt], pend[2], pend[0], ALU.mult, [pend[3], pend[1]], [("hT", pend[4])])

        self.proj_down(l, nsub, P)

    def mlstm_pre(self, l, nch, L):
        n4 = 4 * nch
        fl = lambda t: t[0:L, 0:nch, :].rearrange("p c h -> p (c h)")
        self.act(self.lgE[0:L, 0:nch, :], self.gat[0:L, 0:nch, 4:8], AF.Exp, [("gat", ci) for ci in range(nch)], ["lgE"], scale=-1.0)
        self.act(self.lgF[0:L, 0:nch, :], self.lgE[0:L, 0:nch, :], AF.Ln, ["lgE", "ones"], ["lgF"], bias=self.ones[0:L, 0:1])
        b, bk = self.bank()
        self.mm(b[0:L, 0:n4], self.triu[0:L, 0:L], fl(self.lgF), True, True, ["triu", "lgF"], [bk])
        self.mm(b[:, 16:16 + n4], self.ones[0:L, :], fl(self.lgF), True, True, ["ones", "lgF"], [bk])
        self.tt("dve", fl(self.lgA), self.gat[0:L, 0:nch, 0:4], b[0:L, 0:n4].rearrange("p (c h) -> p c h", h=4), ALU.add,
                [("gat", ci) for ci in range(nch)] + [bk], ["lgA"])
        self.cp("dve", fl(self.clS), b[0:L, 0:n4], [bk], ["clS"])
        self.cp("dve", self.clLs[:, 0:nch, :].rearrange("p c h -> p (c h)"), b[:, 16:16 + n4], [bk], ["clLs"])
        b, bk = self.bank()
        self.tr(b[0:n4, 0:L], fl(self.lgA), self.ident[0:L, 0:L], ["lgA", "ident"], [bk])
        self.S.op("dve", lambda e, b=b: e.tensor_reduce(out=self.amx[0:n4, 0:1], in_=b[0:n4, 0:L], axis=AX.X, op=ALU.max),
                  [bk], ["amx"])
        self.ts("dve", self.amd[0:n4, 0:n4], self.ident[0:n4, 0:n4], self.amx[0:n4, 0:1], None, ALU.mult, None,
                ["ident", "amx"], ["amd"])
        b2, bk2 = self.bank()
        self.mm(b2[:, 0:n4], self.ones[0:n4, :], self.amd[0:n4, 0:n4], True, True, ["ones", "amd"], [bk2])
        self.cp("dve", self.amaxS[:, 0:nch, :].rearrange("p c h -> p (c h)"), b2[:, 0:n4], [bk2],
                [("amaxS", ci) for ci in range(nch)])

    def mlstm_chunk(self, l, ci, off, L):
        ms = self.ms[:, ci, :]
        K = lambda n: ("ms", ci, n)
        qT = lambda h: self.GB1[:, h, off:off + L]
        kT = lambda h: self.GB1[:, 4 + h, off:off + L]
        hb = [(self.psA, ("PS", "A")), (self.psS, ("PS", "S")), (self.psH, ("PS", "H")), (self.psC, ("PS", "C"))]
        mr = self.mrep[l]
        self.tt("dve", ms[:, 0:4], mr[:, :], self.amaxS[:, ci, :], ALU.max, [("mrep", l), ("amaxS", ci)], [K("mu")])
        self.tt("dve", ms[:, 4:8], mr[:, :], ms[:, 0:4], ALU.subtract, [("mrep", l), K("mu")], [K("dm")])
        self.tt("dve", ms[0:L, 8:12], self.lgA[0:L, ci, :], ms[0:L, 0:4], ALU.subtract, ["lgA", K("mu")], [K("ea")])
        self.tt("dve", ms[0:L, 12:16], self.clS[0:L, ci, :], ms[0:L, 0:4], ALU.subtract, ["clS", K("mu")], [K("ea")])
        self.act(ms[:, 16:20], ms[:, 4:8], AF.Exp, [K("dm")], [K("wc")])
        self.act(ms[0:L, 20:28], ms[0:L, 8:16], AF.Exp, [K("ea")], [K("e")])
        self.tt("dve", mr[:, :], ms[:, 0:4], self.clLs[:, ci, :], ALU.subtract, [K("mu"), "clLs"], [("mrep", l)])
        for h in range(4):
            e_h = ms[0:L, 20 + h:21 + h]
            wc_h = ms[:, 16 + h:17 + h]
            Ck = ("Cn", l, h)
            B, Bk = hb[h]
            self.act(self.Csb[:, h, 0:129], self.Cn[l][:, h, :], AF.Copy, [Ck, K("wc")], [("Csb", h)], scale=wc_h)
            self.ts("dve", self.kw[0:L, h, :], self.ktm[0:L, ci, h * 128:(h + 1) * 128], e_h, None, ALU.mult, None,
                    [("ktm", ci), K("e")], [("kw", h)])
            self.mm(B[0:L, 0:L], kT(h), qT(h), True, True, [("GB1", 4 + h), ("GB1", h)], [Bk])
            self.stt(self.smb[0:L, h, 0:L], B[0:L, 0:L], e_h, self.triu[0:L, 0:L], ALU.mult, ALU.mult,
                     [Bk, K("e"), "triu"], [("smb", h)])
            H = B[0:L, 128:257]
            v1h = self.v1[0:L, ci, h, 0:129]
            self.mm(H, self.smb[0:L, h, 0:L], v1h, True, False, [("smb", h), ("v1", ci), "v1all"], [Bk])
            self.mm(H, qT(h), self.Csb[:, h, 0:129], False, True, [("GB1", h), ("Csb", h)], [Bk])
            Cps = B[:, 257:386]
            self.mm(Cps, self.kw[0:L, h, :], v1h, True, True, [("kw", h), ("v1", ci), "v1all"], [Bk])
            self.stt(self.Cn[l][:, h, :], self.Cn[l][:, h, :], wc_h, Cps, ALU.mult, ALU.add, [Ck, K("wc"), Bk], [Ck])
            self.act(self.ubuf[0:L, ci, h * 129:(h + 1) * 129], H, AF.Copy, [Bk], [("ubuf", ci)])

    def mlstm_tail(self, l, nch, L):
        nm = self.nms
        ukeys = [("ubuf", c) for c in range(nch)]
        Hall = self.ubuf[0:L, 0:nch, :].rearrange("p c (h e) -> p c h e", e=129)
        fl = lambda a, b: nm[0:L, 0:nch, a:b]
        den = self.ubuf[0:L, 0:nch, 128:516:129] if False else Hall[:, :, :, 128]
        lower = self.ms[0:L, 0:nch, 24:28]
        mk = [("ms", c, "e") for c in range(nch)]
        for c in range(nch):
            self.stt(nm[0:L, c, 0:4], Hall[:, c, :, 128], -1.0, self.ms[0:L, c, 24:28], ALU.mult, ALU.max,
                     [("ubuf", c), ("ms", c, "e")], [("nms", c, "t1")])
            self.tt("dve", nm[0:L, c, 4:8], Hall[:, c, :, 128], nm[0:L, c, 0:4], ALU.max, [("ubuf", c), ("nms", c, "t1")], [("nms", c, "da")])
            for h in range(4):
                self.act(self.junk[0:L, h * 128:(h + 1) * 128], Hall[:, c, h, 0:128], AF.Square, [("ubuf", c)],
                         [("junkq", h), ("nms", c, ("ssq", h))], accum_out=nm[0:L, c, 12 + h:13 + h])
        da_k = [("nms", c, "da") for c in range(nch)]
        sq_k = [("nms", c, ("ssq", h)) for c in range(nch) for h in range(4)]
        self.recip(fl(8, 12), fl(4, 8), da_k, ["nms_rd"])
        self.tt("dve", fl(16, 20), fl(8, 12), fl(8, 12), ALU.mult, ["nms_rd"], ["nms_r2"])
        self.tt("dve", fl(20, 24), fl(12, 16), fl(16, 20), ALU.mult, sq_k + ["nms_r2"], ["nms_v"])
        self.ts("dve", fl(20, 24), fl(20, 24), 1.0 / 128.0, EPS, ALU.mult, ALU.add, ["nms_v"], ["nms_v2"])
        for c in range(nch):
            self.tt("pool", nm[0:L, c, 24:28], nm[0:L, c, 20:24], self.nhalf[0:L, 0:4], ALU.pow, ["nms_v2", "nhalf"], [("nms", c, "rs")])
        self.tt("dve", fl(28, 32), fl(8, 12), fl(24, 28), ALU.mult, ["nms_rd"] + [("nms", c, "rs") for c in range(nch)], ["nms_sc"])
        for h in range(4):
            self.act(self.so[:, h, 0:nch * L], self.so[:, h, 0:nch * L], AF.Copy, [("so", h), ("colv", l)], [("so", h)],
                     scale=self.cv(l, CV_NW + h))
        for c in range(nch):
            off = c * L
            for h in range(4):
                self.act(self.hn[0:L, h, :], Hall[:, c, h, 0:128], AF.Copy, [("ubuf", c), "nms_sc"], [("hn", h)],
                         scale=nm[0:L, c, 28 + h:29 + h])
            b, bk = self.bank()
            for h in range(4):
                self.tr(b[:, h * L:(h + 1) * L], self.hn[0:L, h, :], self.ident[0:L, 0:L], [("hn", h), "ident"], [bk])
            self.tt("dve", self.GB2[:, 4:8, off:off + L], b[:, 0:4 * L].rearrange("p (h t) -> p h t", h=4),
                    self.so[:, :, off:off + L], ALU.mult, [bk] + [("so", h) for h in range(4)], [("GB2", 4 + h) for h in range(4)])

    def init_sample(self, r):
        d = self.d
        for l in range(DEPTH):
            self.dma("sp", self.Cn[l][:, :, 0:128], d["sC"][l, r].rearrange("h d e -> d h e"), ("ldC", l), [],
                     [("Cn", l, h) for h in range(4)])
            self.dma("sp", self.stA[0:8, :], d["sconv"][l, r].rearrange("j (c p) -> (j c) p", p=128), ("stA", 0), [], ["stA"])
            self.dma("sp", self.stA[8:12, :], d["sn"][l, r], ("stA", 1), [], ["stA"])
            self.dma("sp", self.stB[0:88, :], d["sffn"][l, r].rearrange("j (c p) -> (j c) p", p=128), ("stB", 0), [], ["stB"])
            b, bk = self.bank()
            self.tr(b[:, 0:128], self.stA[:], self.ident[:], ["stA", "ident"], [bk])
            self.tr(b[:, 128:256], self.stB[:], self.ident[:], ["stB", "ident"], [bk])
            self.cp("dve", self.uh[l][:, :, :], b[:, 0:8].rearrange("p (j c) -> p c j", j=2), [bk], [("uh", l)])
            self.cp("dve", self.Cn[l][:, :, 128:129], b[:, 8:12].rearrange("p (h o) -> p h o", o=1), [bk],
                    [("Cn", l, h) for h in range(4)])
            self.cp("dve", self.ffh[l][:, :, :], b[:, 128:216].rearrange("p (j c) -> p c j", j=2), [bk],
                    [("ffh", l, ch) for ch in range(44)])
            self.dma("sp", self.mrep[l][:, :], d["sm"][l, r].partition_broadcast(128), ("ldm", l), [], [("mrep", l)])
            self.dma("pool", self.xn[:, 0:2, :], d["cmk"][l, r].rearrange("(mc p) d -> p mc d", p=128), "ldmk", [],
                     [("xn", 0), ("xn", 1)])
            for g in range(2):
                for cc in range(4):
                    for mc in range(2):
                        c = g * 4 + cc
                        self.tr(self.pstr[:, (cc * 2 + mc) * 128:(cc * 2 + mc + 1) * 128], self.xn[:, mc, c * 128:(c + 1) * 128],
                                self.identb[:, :], [("xn", mc), "identb"], [("PS", "tr")])
                self.cp("dve", self.mkT[l][:, g * 4:(g + 1) * 4, :], self.pstr[:, :].rearrange("p (c m) -> p c m", c=4),
                        [("PS", "tr")], [("mkT", l)])
            self.dma("pool", self.mv[l][:, :, :], d["cmv"][l, r].rearrange("(mc p) d -> p mc d", p=128), ("ldmv", l), [],
                     [("mv", l)])

    def init_prompt(self):
        d = self.d
        for l in range(DEPTH):
            self.memset("dve", self.Cn[l][:], 0.0, [("Cn", l, h) for h in range(4)])
            self.memset("dve", self.mrep[l][:], 0.0, [("mrep", l)])
            self.memset("dve", self.uh[l][:], 0.0, [("uh", l)])
            self.memset("dve", self.ffh[l][:], 0.0, [("ffh", l, ch) for ch in range(44)])
            self.dma("sp", self.ybuf[:, 0:2, :], d["mem"].rearrange("(mc p) d -> p mc d", p=128), "ldmem", [],
                     [("ybuf", 0), ("ybuf", 1)])
            self.norm_T(l, [(self.ybuf[:, mc, :], ("ybuf", mc), 128) for mc in range(2)], NMEM, CV_MEMKV, self.xnT, "xnT")
            for h2 in range(2):
                wt, wk = self.wload(l, U_MK + h2)
                for cc in range(4):
                    b, bk = self.bank()
                    for kc in range(KC):
                        self.mm(b[:, 0:NMEM], wt[:, kc, cc * 128:(cc + 1) * 128], self.xnT[:, kc, 0:NMEM], kc == 0, kc == KC - 1,
                                [wk, ("xnT", kc)], [bk])
                    self.act(self.mkT[l][:, h2 * 4 + cc, :], b[:, 0:NMEM], AF.Copy, [bk], [("mkT", l)])
                for mc in range(2):
                    b, bk = self.bank()
                    for kc in range(KC):
                        self.mm(b[:, :], self.xnT[:, kc, mc * 128:(mc + 1) * 128], wt[:, kc, :], kc == 0, kc == KC - 1,
                                [wk, ("xnT", kc)], [bk])
                    self.act(self.G2[:, mc, 0:512], b[:, :], AF.Copy, [bk], [("G2", mc)])
                    self.dma("sp", d["o_mk_p"][l, mc * 128:(mc + 1) * 128, h2 * 512:(h2 + 1) * 512], self.G2[:, mc, 0:512],
                             ("omk", mc), [("G2", mc)], [])
            for h2 in range(2):
                wt, wk = self.wload(l, U_MV + h2)
                for mc in range(2):
                    b, bk = self.bank()
                    for kc in range(KC):
                        self.mm(b[:, :], self.xnT[:, kc, mc * 128:(mc + 1) * 128], wt[:, kc, :], kc == 0, kc == KC - 1,
                                [wk, ("xnT", kc)], [bk])
                    self.act(self.G2[:, 2 + mc, 0:512], b[:, :], AF.Copy, [bk], [("G2", 2 + mc)])
                    self.cp("dve", self.mv[l][:, mc, h2 * 512:(h2 + 1) * 512], b[:, :], [bk], [("mv", l)])
                    self.dma("sp", d["o_mv_p"][l, mc * 128:(mc + 1) * 128, h2 * 512:(h2 + 1) * 512], self.G2[:, 2 + mc, 0:512],
                             ("omv", mc), [("G2", 2 + mc)], [])

    def finalize(self, kind, r):
        d = self.d
        sfx = "_s" if kind == "s" else "_p"
        for l in range(DEPTH):
            Ck = [("Cn", l, h) for h in range(4)]
            self.dma("sp", d["o_C" + sfx][l, r].rearrange("h d e -> d h e"), self.Cn[l][:, :, 0:128], ("oC", l), Ck, [])
            self.cp("dve", self.tail[:, 0:8].rearrange("p (j c) -> p c j", j=2), self.uh[l][:, :, :], [("uh", l)], ["tail"])
            self.cp("dve", self.tail[:, 8:12].rearrange("p (h o) -> p h o", o=1), self.Cn[l][:, :, 128:129], Ck, ["tail"])
            self.cp("dve", self.tail[:, 32:120].rearrange("p (j c) -> p c j", j=2), self.ffh[l][:, :, :],
                    [("ffh", l, ch) for ch in range(44)], ["tail"])
            b, bk = self.bank()
            self.tr(b[0:12, 0:128], self.tail[:, 0:12], self.ident[:, :], ["tail", "ident"], [bk])
            self.tr(b[0:88, 128:256], self.tail[:, 32:120], self.ident[:, :], ["tail", "ident"], [bk])
            self.cp("dve", self.tailr[0:12, :], b[0:12, 0:128], [bk], ["tailr"])
            self.cp("dve", self.tailr2[0:88, :], b[0:88, 128:256], [bk], ["tailr2"])
            self.dma("sp", d["o_conv" + sfx][l, r].rearrange("j (c p) -> (j c) p", p=128), self.tailr[0:8, :], "otail", ["tailr"], [])
            self.dma("sp", d["o_n" + sfx][l, r], self.tailr[8:12, :], "otail", ["tailr"], [])
            self.dma("sp", d["o_ffn" + sfx][l, r].rearrange("j (c p) -> (j c) p", p=128), self.tailr2[0:88, :], "otail2", ["tailr2"], [])
            self.dma("sp", d["o_m" + sfx][l, r:r + 1, :], self.mrep[l][0:1, :], ("om", l), [("mrep", l)], [])

    def build(self):
        d_ = self.declare()
        self.alloc()
        self.consts()
        for l in range(DEPTH):
            self.convert(l)
        d = self.d
        for r in range(2):
            self.init_sample(r)
            self.dma("sp", self.x[0:64, 0, :], d["xs"][r], "ldx", [], [("x", 0)])
            for l in range(DEPTH):
                self.layer(l, 64, 64)
            self.dma("sp", d["ys"][r], self.x[0:64, 0, :], "sty", [("x", 0)], [])
            self.finalize("s", r)
        if self.TP:
            self.init_prompt()
            ns = NTP // 128
            for ti in range(self.TP // NTP):
                src = d["xp"][ti * NTP:(ti + 1) * NTP, :].rearrange("(s p) d -> p s d", p=128)
                dst = d["yp"][ti * NTP:(ti + 1) * NTP, :].rearrange("(s p) d -> p s d", p=128)
                for s in range(ns):
                    self.dma("sp", self.x[:, s, :], src[:, s, :], ("ldx", s), [], [("x", s)])
                for l in range(DEPTH):
                    self.layer(l, NTP, 128)
                for s in range(ns):
                    self.dma("sp", dst[:, s, :], self.x[:, s, :], ("sty", s), [("x", s)], [])
            self.finalize("p", 0)
        self.S.emit(self.nc)
        self.st.close()
        return self.nc


_WNAMES = ("norm_mix_pre", "w_in", "b_gates", "conv_w", "mlstm_norm_w", "w_out", "norm_mix_post",
           "norm_mem_pre", "norm_mem_kv", "w_mq", "w_mk", "w_mv", "w_mo", "norm_mem_post",
           "norm_ffn_pre", "w_up", "ffn_conv_w", "w_down", "norm_ffn_post")

TP_PER_CORE = 16384
PROMPT_CORES = (0, 4)


def kernel(**inputs):
    f = lambda a: np.ascontiguousarray(np.asarray(a, dtype=np.float32))
    inp = {k: f(v) for k, v in inputs.items()}
    TP = TP_PER_CORE
    nc = Builder(TP).build()
    in_maps = []
    for c in range(NCORES):
        sl = slice(2 * c, 2 * c + 2)
        m = {
            "xs": inp["x_sample"][sl],
            "cmk": f(inp["cache_mem_k"][:, sl].reshape(DEPTH, 2, NMEM, D)),
            "cmv": f(inp["cache_mem_v"][:, sl].reshape(DEPTH, 2, NMEM, D)),
            "sconv": f(inp["state_conv"][:, sl]),
            "sC": f(inp["state_mlstm_C"][:, sl]),
            "sn": f(inp["state_mlstm_n"][:, sl]),
            "sm": f(inp["state_mlstm_m"][:, sl]),
            "sffn": f(inp["state_ffn_conv"][:, sl]),
        }
        if TP and c in PROMPT_CORES:
            pb = PROMPT_CORES.index(c)
            m["mem"] = inp["mem_prompt"][pb]
            m["xp"] = f(inp["x_prompt"][pb, 0:TP])
        else:
            m["mem"] = np.zeros((NMEM, D), np.float32)
            m["xp"] = np.zeros((max(TP, NTP), D), np.float32)
        for n in _WNAMES:
            m[n] = inp[n]
        in_maps.append(m)
    res = run_bass_kernel_spmd(nc, in_maps, core_ids=list(range(NCORES))).results
    B = inp["x_prompt"].shape[0]
    y_p = np.zeros((B, PSEQ, D), np.float32)
    mk_p = np.zeros((DEPTH, B, NMEM, 4, 256), np.float32)
    mv_p = np.zeros_like(mk_p)
    conv_p = np.zeros((DEPTH, B, 2, 512), np.float32)
    C_p = np.zeros((DEPTH, B, 4, 128, 128), np.float32)
    n_p = np.zeros((DEPTH, B, 4, 128), np.float32)
    m_p = np.zeros((DEPTH, B, 4), np.float32)
    ff_p = np.zeros((DEPTH, B, 2, 2 * DFF), np.float32)
    if TP:
        for b in range(B):
            r = res[PROMPT_CORES[b]]
            y_p[b, 0:TP] = r["yp"][0:TP]
            mk_p[:, b] = r["o_mk_p"].reshape(DEPTH, NMEM, 4, 256)
            mv_p[:, b] = r["o_mv_p"].reshape(DEPTH, NMEM, 4, 256)
            conv_p[:, b] = r["o_conv_p"][:, 0]
            C_p[:, b] = r["o_C_p"][:, 0]
            n_p[:, b] = r["o_n_p"][:, 0]
            m_p[:, b] = r["o_m_p"][:, 0]
            ff_p[:, b] = r["o_ffn_p"][:, 0]
    cat = lambda k, ax: np.concatenate([res[c][k] for c in range(NCORES)], axis=ax)
    y_s = cat("ys", 0)
    conv_s = cat("o_conv_s", 1)
    C_s = cat("o_C_s", 1)
    n_s = cat("o_n_s", 1)
    m_s = cat("o_m_s", 1)
    ff_s = cat("o_ffn_s", 1)
    return (y_p, y_s, mk_p, mv_p, conv_p, conv_s, C_p, C_s, n_p, n_s, m_p, m_s, ff_p, ff_s)
```

```python
import numpy as np
from contextlib import ExitStack
import concourse.bass as bass
import concourse.mybir as mybir
from concourse.bass_utils import run_bass_kernel_spmd

F32 = mybir.dt.float32
BF16 = mybir.dt.bfloat16
AF = mybir.ActivationFunctionType
ALU = mybir.AluOpType
AX = mybir.AxisListType

DEPTH = 2
D = 1024
KC = 8
DFF = 2816
NJ = 22
NMEM = 256
EPS = 1e-6
NCORES = 8
PSEQ = 16384
NTP = 512
NSLOT = 4
NU = 34
U_XC, U_B, U_C, U_Q, U_K, U_V, U_O = 0, 1, 2, 3, 4, 5, 6
U_OUT, U_MQ, U_MO, U_UP, U_DN, U_MK, U_MV = 7, 9, 11, 13, 24, 30, 32
DN_GROUPS = ((0, 8), (8, 8), (16, 6))
CV_MIXPRE, CV_MEMPRE, CV_MEMKV, CV_FFNPRE, CV_CONVW, CV_NW, CV_FCW = 0, 8, 16, 24, 32, 44, 48


class _Op:
    __slots__ = ("eng", "fn", "r", "w", "is_dma", "dkey", "dval", "signal", "sigcount",
                 "waits", "idx")


class Sched:
    ENGS = ("pe", "act", "dve", "pool", "sp")

    def __init__(self):
        self.ops = []
        self.eops = {e: [] for e in self.ENGS}
        self.last_w = {}
        self.readers = {}
        self.dma_cnt = {}
        self.dma_last = {}
        self.tags = {}
        self.phase = ""
        self.inst_tag = {}

    def _add(self, op):
        op.idx = len(self.ops)
        self.tags[op.idx] = self.phase
        op.signal = False
        op.sigcount = None
        op.waits = []
        deps = []
        psr = tuple(k for k in op.r if isinstance(k, tuple) and k and k[0] == "PS")
        if psr:
            op.r = tuple(k for k in op.r if k not in psr)
            op.w = tuple(op.w) + psr
        for k in op.r:
            p = self.last_w.get(k)
            if p is not None:
                deps.append((p, "raw"))
        for k in op.w:
            p = self.last_w.get(k)
            if p is not None:
                deps.append((p, "waw"))
            for q in self.readers.get(k, ()):
                deps.append((q, "war"))
        if op.is_dma:
            p = self.dma_last.get(op.dkey)
            if p is not None:
                deps.append((p, "raw"))
        latest = {}
        for p, kind in deps:
            if p is not op and not p.is_dma:
                q = latest.get(p.eng)
                if q is None or p.idx > q.idx:
                    latest[p.eng] = p
        deps = [(p, kind) for p, kind in deps if p.is_dma or latest.get(p.eng) is p]
        seen = set()
        for p, kind in deps:
            if p is op:
                continue
            need = True
            if (not p.is_dma) and (not op.is_dma) and p.eng == op.eng:
                need = (kind == "raw") and op.eng != "pe"
            if need and p.idx not in seen:
                seen.add(p.idx)
                op.waits.append(p)
                if not p.is_dma:
                    p.signal = True
        for k in op.r:
            self.readers.setdefault(k, []).append(op)
        for k in op.w:
            self.last_w[k] = op
            self.readers[k] = []
        if op.is_dma:
            self.dma_cnt[op.dkey] = self.dma_cnt.get(op.dkey, 0) + 16
            op.dval = self.dma_cnt[op.dkey]
            self.dma_last[op.dkey] = op
        self.ops.append(op)
        self.eops[op.eng].append(op)
        return op

    def op(self, eng, fn, r=(), w=()):
        o = _Op()
        o.eng, o.fn, o.r, o.w = eng, fn, tuple(r), tuple(w)
        o.is_dma, o.dkey, o.dval = False, None, None
        return self._add(o)

    def dma(self, eng, fn, key, r=(), w=()):
        o = _Op()
        o.eng, o.fn, o.r, o.w = eng, fn, tuple(r), tuple(w)
        o.is_dma, o.dkey, o.dval = True, key, None
        return self._add(o)

    def emit(self, nc):
        for e in self.ENGS:
            c = 0
            for o in self.eops[e]:
                if o.signal:
                    c += 1
                    o.sigcount = c
        with ExitStack() as st:
            esem = {e: st.enter_context(nc.semaphore("s_" + e)) for e in self.ENGS}
            dsem = {}
            for i, k in enumerate(self.dma_cnt):
                dsem[k] = st.enter_context(nc.semaphore("d%d" % i))
            block = st.enter_context(nc.Block())

            def run(e, engobj):
                waited = {}
                for o in self.eops[e]:
                    need = {}
                    for p in o.waits:
                        if p.is_dma:
                            sem, val, sk = dsem[p.dkey], p.dval, ("d", p.dkey)
                        else:
                            sem, val, sk = esem[p.eng], p.sigcount, ("e", p.eng)
                        if sk not in need or need[sk][1] < val:
                            need[sk] = (sem, val)
                    for sk, (sem, val) in need.items():
                        if waited.get(sk, 0) < val:
                            engobj.wait_ge(sem, val)
                            waited[sk] = val
                    inst = o.fn(engobj)
                    try:
                        self.inst_tag[inst.ins.name] = self.tags[o.idx]
                    except Exception:
                        pass
                    if o.is_dma:
                        inst.then_inc(dsem[o.dkey], 16)
                    elif o.signal:
                        inst.then_inc(esem[e], 1)
                if e == "sp":
                    for k, v in self.dma_cnt.items():
                        engobj.wait_ge(dsem[k], v)

            block.tensor(lambda t: run("pe", t))
            block.scalar(lambda t: run("act", t))
            block.vector(lambda t: run("dve", t))
            block.gpsimd(lambda t: run("pool", t))
            block.sync(lambda t: run("sp", t))


class Builder:
    def __init__(self, tp):
        self.TP = tp
        self.nc = bass.Bass("TRN2", target_bir_lowering=False)
        self.S = Sched()
        self.st = ExitStack()
        self.rot = 0
        self.ring_seq = 0
        self.cv_rr = 0

    def mm(self, out, lhsT, rhs, start, stop, r, w):
        self.S.op("pe", lambda e: e.matmul(out, lhsT=lhsT, rhs=rhs, start=start, stop=stop), r, w)

    def tr(self, out, in_, ident, r, w):
        self.S.op("pe", lambda e: e.transpose(out, in_=in_, identity=ident), r, w)

    def act(self, out, in_, func, r, w, **kw):
        self.S.op("act", lambda e: e.activation(out=out, in_=in_, func=func, **kw), r, w)

    def tt(self, eng, out, in0, in1, op, r, w):
        self.S.op(eng, lambda e: e.tensor_tensor(out=out, in0=in0, in1=in1, op=op), r, w)

    def ts(self, eng, out, in0, s1, s2, op0, op1, r, w):
        if op1 is None:
            self.S.op(eng, lambda e: e.tensor_scalar(out=out, in0=in0, scalar1=s1, scalar2=None, op0=op0), r, w)
        else:
            self.S.op(eng, lambda e: e.tensor_scalar(out=out, in0=in0, scalar1=s1, scalar2=s2, op0=op0, op1=op1), r, w)

    def stt(self, out, in0, scalar, in1, op0, op1, r, w):
        self.S.op("dve", lambda e: e.scalar_tensor_tensor(out=out, in0=in0, scalar=scalar, in1=in1, op0=op0, op1=op1), r, w)

    def cp(self, eng, out, in_, r, w):
        self.S.op(eng, lambda e: e.tensor_copy(out=out, in_=in_), r, w)

    def memset(self, eng, ap, val, w):
        self.S.op(eng, lambda e: e.memset(ap, val), (), w)

    def recip(self, out, in_, r, w):
        self.S.op("dve", lambda e: e.reciprocal(out=out, in_=in_), r, w)

    def dma(self, eng, out, in_, key, r, w):
        self.S.dma(eng, lambda e: e.dma_start(out=out, in_=in_), key, r, w)

    def sb(self, name, shape, dt):
        return self.st.enter_context(self.nc.sbuf_tensor(name, shape, dt))

    def bank(self):
        b = self.rot % 3
        self.rot += 1
        return self.ps[b], ("PS", b)

    def declare(self):
        nc = self.nc
        I = lambda n, s: nc.dram_tensor(n, s, F32, kind="ExternalInput").ap()
        O = lambda n, s: nc.dram_tensor(n, s, F32, kind="ExternalOutput").ap()
        TP = max(self.TP, NTP)
        d = {}
        d["xs"] = I("xs", [2, 64, D])
        d["cmk"] = I("cmk", [DEPTH, 2, NMEM, D])
        d["cmv"] = I("cmv", [DEPTH, 2, NMEM, D])
        d["sconv"] = I("sconv", [DEPTH, 2, 2, 512])
        d["sC"] = I("sC", [DEPTH, 2, 4, 128, 128])
        d["sn"] = I("sn", [DEPTH, 2, 4, 128])
        d["sm"] = I("sm", [DEPTH, 2, 4])
        d["sffn"] = I("sffn", [DEPTH, 2, 2, 2 * DFF])
        d["xp"] = I("xp", [TP, D])
        d["mem"] = I("mem", [NMEM, D])
        for n in ("norm_mix_pre", "norm_mix_post", "norm_mem_pre", "norm_mem_kv", "norm_mem_post",
                  "norm_ffn_pre", "norm_ffn_post"):
            d[n] = I(n, [DEPTH, D])
        d["w_in"] = I("w_in", [DEPTH, D, 3592])
        d["b_gates"] = I("b_gates", [DEPTH, 8])
        d["conv_w"] = I("conv_w", [DEPTH, 3, 512])
        d["mlstm_norm_w"] = I("mlstm_norm_w", [DEPTH, 512])
        for n in ("w_out", "w_mq", "w_mk", "w_mv", "w_mo"):
            d[n] = I(n, [DEPTH, D, D])
        d["w_up"] = I("w_up", [DEPTH, D, 2 * DFF])
        d["ffn_conv_w"] = I("ffn_conv_w", [DEPTH, 3, 2 * DFF])
        d["w_down"] = I("w_down", [DEPTH, DFF, D])
        d["ys"] = O("ys", [2, 64, D])
        d["o_conv_s"] = O("o_conv_s", [DEPTH, 2, 2, 512])
        d["o_C_s"] = O("o_C_s", [DEPTH, 2, 4, 128, 128])
        d["o_n_s"] = O("o_n_s", [DEPTH, 2, 4, 128])
        d["o_m_s"] = O("o_m_s", [DEPTH, 2, 4])
        d["o_ffn_s"] = O("o_ffn_s", [DEPTH, 2, 2, 2 * DFF])
        d["yp"] = O("yp", [TP, D])
        d["o_mk_p"] = O("o_mk_p", [DEPTH, NMEM, D])
        d["o_mv_p"] = O("o_mv_p", [DEPTH, NMEM, D])
        d["o_conv_p"] = O("o_conv_p", [DEPTH, 1, 2, 512])
        d["o_C_p"] = O("o_C_p", [DEPTH, 1, 4, 128, 128])
        d["o_n_p"] = O("o_n_p", [DEPTH, 1, 4, 128])
        d["o_m_p"] = O("o_m_p", [DEPTH, 1, 4])
        d["o_ffn_p"] = O("o_ffn_p", [DEPTH, 1, 2, 2 * DFF])
        d["wscr"] = nc.dram_tensor("wscr", [DEPTH, NU, 128, KC * 512], BF16, kind="Internal").ap()
        self.d = d

    def alloc(self):
        sb = self.sb
        NS = NTP // 128
        self.x = sb("x", [128, NS, D], F32)
        self.xn = sb("xn", [128, NS, D], BF16)
        self.xnT = sb("xnT", [128, KC, NTP], BF16)
        self.junk = sb("junk", [128, D], BF16)
        self.junk2 = sb("junk2", [128, D], BF16)
        self.ybuf = sb("ybuf", [128, NS, D], F32)
        self.G1 = sb("G1", [128, 4, 516], F32)
        self.G2 = sb("G2", [128, 4, 516], F32)
        self.so = sb("so", [128, 4, NTP], BF16)
        self.ubuf = sb("ubuf", [128, 4, 516], F32)
        self.GB1 = sb("GB1", [128, KC, NTP], BF16)
        self.GB2 = sb("GB2", [128, KC, NTP], BF16)
        self.ktm = sb("ktm", [128, NS, 512], BF16)
        self.v1 = sb("v1", [128, NS, 4, 130], BF16)
        self.gat = sb("gat", [128, NS, 8], F32)
        self.hT = sb("hT", [128, NJ, NTP], BF16)
        self.ring = [sb("ring%d" % i, [128, KC, 512], BF16) for i in range(NSLOT)]
        self.mkT = [sb("mkT%d" % l, [128, KC, NMEM], BF16) for l in range(DEPTH)]
        self.mv = [sb("mv%d" % l, [128, 2, D], BF16) for l in range(DEPTH)]
        self.grep = sb("grep", [128, D], F32)
        self.colv = sb("colv", [128, DEPTH, 256], F32)
        self.brep = sb("brep", [128, DEPTH, 8], F32)
        self.wg = sb("wg", [128, DEPTH, KC, 8], BF16)
        self.ident = sb("ident", [128, 128], F32)
        self.identb = sb("identb", [128, 128], BF16)
        self.ones = sb("ones", [128, 128], F32)
        self.onesb = sb("onesb", [128, 128], BF16)
        self.triu = sb("triu", [128, 128], F32)
        self.stA = sb("stA", [128, 128], F32)
        self.stB = sb("stB", [128, 128], F32)
        self.Cn = [sb("Cn%d" % l, [128, 4, 129], F32) for l in range(DEPTH)]
        self.mrep = [sb("mrep%d" % l, [128, 4], F32) for l in range(DEPTH)]
        self.uh = [sb("uh%d" % l, [128, 4, 2], F32) for l in range(DEPTH)]
        self.ffh = [sb("ffh%d" % l, [128, 44, 2], F32) for l in range(DEPTH)]
        self.nhalf = sb("nhalf", [128, 8], F32)
        self.tail = sb("tail", [128, 128], F32)
        self.tailr = sb("tailr", [128, 128], F32)
        self.tailr2 = sb("tailr2", [128, 128], F32)
        self.stat = sb("stat", [128, 64], F32)
        self.lgE = sb("lgE", [128, NS, 4], F32)
        self.lgF = sb("lgF", [128, NS, 4], F32)
        self.lgA = sb("lgA", [128, NS, 4], F32)
        self.clS = sb("clS", [128, NS, 4], F32)
        self.clLs = sb("clLs", [128, NS, 4], F32)
        self.amaxS = sb("amaxS", [128, NS, 4], F32)
        self.Rm = sb("Rm", [128, 4, 128], F32)
        self.ms = sb("ms", [128, NS, 32], F32)
        self.nms = sb("nms", [128, NS, 32], F32)
        self.amx = sb("amx", [16, 1], F32)
        self.amd = sb("amd", [16, 16], F32)
        self.Csb = sb("Csb", [128, 4, 130], BF16)
        self.kw = sb("kw", [128, 4, 128], BF16)
        self.smb = sb("smb", [128, 4, 128], BF16)
        self.hn = sb("hn", [128, 4, 128], F32)
        ps = self.st.enter_context
        nc = self.nc
        self.ps = [ps(nc.psum_tensor("psr%d" % i, [128, 512], F32)) for i in range(3)]
        self.pstr = ps(nc.psum_tensor("pstr", [128, 1024], BF16))
        self.psA = ps(nc.psum_tensor("psA", [128, 512], F32))
        self.pstr2 = self.psA[:, :].bitcast(BF16)
        self.psS = ps(nc.psum_tensor("psS", [128, 512], F32))
        self.psH = ps(nc.psum_tensor("psH", [128, 512], F32))
        self.psC = ps(nc.psum_tensor("psC", [128, 512], F32))

    def consts(self):
        d = self.d
        S = self.S
        self.memset("pool", self.ones[:], 1.0, ["ones"])
        self.memset("pool", self.ident[:], 1.0, ["ident"])
        S.op("pool", lambda e: e.affine_select(out=self.ident[:], in_=self.ident[:], pattern=[[-1, 128]],
                                               compare_op=ALU.is_equal, fill=0.0, base=0, channel_multiplier=1),
             ["ident"], ["ident"])
        self.memset("pool", self.triu[:], 1.0, ["triu"])
        S.op("pool", lambda e: e.affine_select(out=self.triu[:], in_=self.triu[:], pattern=[[1, 128]],
                                               compare_op=ALU.is_ge, fill=0.0, base=0, channel_multiplier=-1),
             ["triu"], ["triu"])
        self.cp("dve", self.identb[:], self.ident[:], ["ident"], ["identb"])
        self.cp("dve", self.onesb[:], self.ones[:], ["ones"], ["onesb"])
        self.memset("dve", self.v1[:], 1.0, ["v1all"])
        self.memset("dve", self.Csb[:], 0.0, ["Csb"])
        self.memset("dve", self.stB[:], 0.0, ["stB"])
        self.memset("dve", self.stA[:], 0.0, ["stA"])
        self.memset("dve", self.tail[:], 0.0, ["tail"])
        self.memset("dve", self.nhalf[:], -0.5, ["nhalf"])
        for l in range(DEPTH):
            rows = [(d["norm_mix_pre"][l].rearrange("(c p) -> c p", p=128), 8),
                    (d["norm_mem_pre"][l].rearrange("(c p) -> c p", p=128), 8),
                    (d["norm_mem_kv"][l].rearrange("(c p) -> c p", p=128), 8),
                    (d["norm_ffn_pre"][l].rearrange("(c p) -> c p", p=128), 8),
                    (d["conv_w"][l].rearrange("j (c p) -> (j c) p", p=128), 12),
                    (d["mlstm_norm_w"][l].rearrange("(c p) -> c p", p=128), 4)]
            r0 = 0
            for i, (ap, n) in enumerate(rows):
                self.dma("sp", self.stA[r0:r0 + n, :], ap, ("stA", i), [], ["stA"])
                r0 += n
            fcw = d["ffn_conv_w"][l].rearrange("j (c p) -> (j c) p", p=128)
            self.dma("sp", self.stA[48:128, :], fcw[0:80, :], ("stA", 6), [], ["stA"])
            self.dma("sp", self.stB[0:52, :], fcw[80:132, :], ("stB", 0), [], ["stB"])
            b, bk = self.bank()
            self.tr(b[:, 0:128], self.stA[:], self.ident[:], ["stA", "ident"], [bk])
            self.tr(b[:, 128:256], self.stB[:], self.ident[:], ["stB", "ident"], [bk])
            self.cp("dve", self.colv[:, l, :], b[:, 0:256], [bk], [("colv", l)])
            self.dma("sp", self.brep[:, l, :], d["b_gates"][l].partition_broadcast(128), ("brep", l), [], [("brep", l)])
            self.dma("pool", self.wg[:, l, :, :],
                     d["w_in"][l][:, 3584:3592].rearrange("(kc p) f -> p kc f", p=128),
                     ("wg", l), [], [("wg", l)])

    def cv(self, l, col):
        return self.colv[:, l, col:col + 1]

    def convert(self, l, only=None):
        d = self.d

        def scr(u):
            return d["wscr"][l, u].rearrange("p (kc f) -> p kc f", kc=KC)

        def cv(dst, src, u):
            key = ("cv", self.cv_rr % 6)
            self.cv_rr += 1
            self.dma("pool", dst, src, key, [], [("wscr", l, u)])

        def colunit(w, c0):
            return w[:, c0:c0 + 512].rearrange("(kc p) f -> p kc f", p=128)

        units = []
        for i in range(7):
            units.append((i, [(scr(i), colunit(d["w_in"][l], i * 512))]))
        for h in range(2):
            units.append((U_OUT + h, [(scr(U_OUT + h), colunit(d["w_out"][l], h * 512))]))
            units.append((U_MQ + h, [(scr(U_MQ + h), colunit(d["w_mq"][l], h * 512))]))
            units.append((U_MO + h, [(scr(U_MO + h), colunit(d["w_mo"][l], h * 512))]))
            units.append((U_MK + h, [(scr(U_MK + h), colunit(d["w_mk"][l], h * 512))]))
            units.append((U_MV + h, [(scr(U_MV + h), colunit(d["w_mv"][l], h * 512))]))
        for u in range(11):
            wu = d["w_up"][l]
            a = wu[:, 2 * u * 128:2 * u * 128 + 256].rearrange("(kc p) f -> p kc f", p=128)
            g = wu[:, DFF + 2 * u * 128:DFF + 2 * u * 128 + 256].rearrange("(kc p) f -> p kc f", p=128)
            units.append((U_UP + u, [(scr(U_UP + u)[:, :, 0:256], a), (scr(U_UP + u)[:, :, 256:512], g)]))
        for h in range(2):
            for gi, (j0, nj) in enumerate(DN_GROUPS):
                src = d["w_down"][l][j0 * 128:(j0 + nj) * 128, h * 512:(h + 1) * 512].rearrange("(j p) f -> p j f", p=128)
                units.append((U_DN + h * 3 + gi, [(scr(U_DN + h * 3 + gi)[:, 0:nj, :], src)]))
        units.sort(key=lambda t: t[0])
        for u, lst in units:
            if only is not None and u not in only:
                continue
            for dst, src in lst:
                cv(dst, src, u)

    def wload(self, l, u, nj=KC):
        slot = self.ring_seq % NSLOT
        self.ring_seq += 1
        t = self.ring[slot]
        src = self.d["wscr"][l, u].rearrange("p (kc f) -> p kc f", kc=KC)
        self.dma("sp", t[:, 0:nj, :], src[:, 0:nj, :], ("ring", slot), [("wscr", l, u)], [("ring", slot)])
        return t, ("ring", slot)

    def norm_T(self, l, srcs, nt, cvbase, dst, dkey, ncols=None):
        prevphase = self.S.phase
        self.S.phase = "norm"
        st = self.stat
        off = 0
        offs = []
        for s, (ap, key, P) in enumerate(srcs):
            if s % 2 == 0:
                self.act(self.junk[0:P, :], ap, AF.Square, [key], ["junk", ("st", s)], accum_out=st[0:P, s:s + 1])
            else:
                self.S.op("dve", lambda e, ap=ap, P=P, s=s: e.scalar_tensor_tensor(
                    out=self.junk2[0:P, :], in0=ap, scalar=1.0, in1=ap, op0=ALU.mult, op1=ALU.mult,
                    accum_out=st[0:P, s:s + 1]), [key], ["junk2", ("st", s)])
            self.ts("pool", st[0:P, 8 + s:9 + s], st[0:P, s:s + 1], 1.0 / D, EPS, ALU.mult, ALU.add,
                    [("st", s)], [("st", 8 + s)])
            self.tt("pool", st[0:P, 24 + s:25 + s], st[0:P, 8 + s:9 + s], self.nhalf[0:P, 0:1], ALU.pow,
                    [("st", 8 + s), "nhalf"], [("st", 24 + s)])
            self.ts("dve", self.xn[0:P, s, :], ap, st[0:P, 24 + s:25 + s], None, ALU.mult, None,
                    [key, ("st", 24 + s)], [("xn", s)])
            offs.append((off, P))
            off += P
        for c in range(KC):
            tb, tk = (self.pstr, ("PS", "tr")) if c % 2 == 0 else (self.pstr2, ("PS", "A"))
            for s, (o, P) in enumerate(offs):
                self.tr(tb[:, o:o + P], self.xn[0:P, s, c * 128:(c + 1) * 128],
                        self.identb[0:P, 0:P], [("xn", s), "identb"], [tk])
            self.act(dst[:, c, 0:nt], tb[:, 0:nt], AF.Copy, [tk, ("colv", l)],
                     [(dkey, c)], scale=self.cv(l, cvbase + c))
        self.S.phase = prevphase

    def load_grep(self, name, l):
        self.dma("sp", self.grep[:], self.d[name][l].partition_broadcast(128), "grep", [], ["grep"])

    def post_evac(self, b, bk, s, h, P):
        st = self.stat
        c = 32 + s * 2 + h
        self.act(self.junk[0:P, h * 512:(h + 1) * 512], b[0:P, :], AF.Square, [bk], [("junkh", h), ("st", c)],
                 accum_out=st[0:P, c:c + 1])
        self.tt("dve", self.ybuf[0:P, s, h * 512:(h + 1) * 512], b[0:P, :], self.grep[0:P, h * 512:(h + 1) * 512],
                ALU.mult, [bk, "grep"], [("ybuf", s)])

    def post_chain(self, nsub, P):
        st = self.stat
        for s in range(nsub):
            c = 32 + s * 2
            self.tt("pool", st[0:P, 40 + s:41 + s], st[0:P, c:c + 1], st[0:P, c + 1:c + 2], ALU.add,
                    [("st", c), ("st", c + 1)], [("st", 40 + s)])
            self.ts("pool", st[0:P, 44 + s:45 + s], st[0:P, 40 + s:41 + s], 1.0 / D, EPS, ALU.mult, ALU.add,
                    [("st", 40 + s)], [("st", 44 + s)])
            self.tt("pool", st[0:P, 52 + s:53 + s], st[0:P, 44 + s:45 + s], self.nhalf[0:P, 0:1], ALU.pow,
                    [("st", 44 + s), "nhalf"], [("st", 52 + s)])
            self.stt(self.x[0:P, s, :], self.ybuf[0:P, s, :], st[0:P, 52 + s:53 + s], self.x[0:P, s, :],
                     ALU.mult, ALU.add, [("ybuf", s), ("st", 52 + s), ("x", s)], [("x", s)])

    def proj_post(self, lhsT_fn, lkey_fn, unit_fn, nsub, P):
        self.S.phase = "post"
        for h in range(2):
            units = unit_fn(h)
            steps = []
            for (wt, wk, j0, nj) in units:
                for j in range(nj):
                    steps.append((wt, wk, j0 + j, j))
            for s in range(nsub):
                b, bk = self.bank()
                for i, (wt, wk, jg, jl) in enumerate(steps):
                    self.mm(b[0:P, :], lhsT_fn(jg, s), wt[:, jl, :], i == 0, i == len(steps) - 1,
                            [lkey_fn(jg), wk], [bk])
                self.post_evac(b, bk, s, h, P)
        self.post_chain(nsub, P)

    def proj_down(self, l, nsub, P):
        self.S.phase = "post"
        hb = [(self.psA, ("PS", "A")), (self.psS, ("PS", "S")), (self.psH, ("PS", "H")), (self.psC, ("PS", "C"))]
        ng = len(DN_GROUPS)
        for h in range(2):
            for gi, (j0, nj) in enumerate(DN_GROUPS):
                wt, wk = self.wload(l, U_DN + h * 3 + gi, nj)
                for s in range(nsub):
                    B, Bk = hb[s]
                    for jl in range(nj):
                        j = j0 + jl
                        self.mm(B[0:P, :], self.hT[:, j, s * 128:s * 128 + P], wt[:, jl, :],
                                gi == 0 and jl == 0, gi == ng - 1 and jl == nj - 1, [("hT", j), wk], [Bk])
            for s in range(nsub):
                B, Bk = hb[s]
                self.post_evac(B, Bk, s, h, P)
        self.post_chain(nsub, P)

    def layer(self, l, nt, L):
        stage = 99
        if stage < 1:
            return
        nsub = max(nt // 128, 1)
        P = min(nt, 128)
        nch = nt // L
        qT = lambda h: self.GB1[:, h, :]
        kT = lambda h: self.GB1[:, 4 + h, :]
        catT = self.GB2
        xcb, Bb = self.G1, self.G2
        xsrc = [(self.x[0:P, s, :], ("x", s), P) for s in range(nsub)]
        self.load_grep("norm_mix_post", l)
        self.norm_T(l, xsrc, nt, CV_MIXPRE, self.xnT, "xnT")
        if stage < 2:
            return
        self.cp("pool", self.ubuf[:, :, 0:2], self.uh[l][:, :, :], [("uh", l)], [("ubufh",)] + [("ubuf", c) for c in range(4)])
        self.S.phase = "win"

        def fm_unit(u, evac):
            wt, wk = self.wload(l, u)
            for cc in range(4):
                b, bk = self.bank()
                for kc in range(KC):
                    self.mm(b[:, 0:nt], wt[:, kc, cc * 128:(cc + 1) * 128], self.xnT[:, kc, 0:nt], kc == 0, kc == KC - 1,
                            [wk, ("xnT", kc)], [bk])
                evac(cc, b, bk)
            return wt, wk

        fm_unit(U_XC, lambda cc, b, bk: self.act(xcb[:, cc, 0:nt], b[:, 0:nt], AF.Copy, [bk], [("G1", cc)]))
        fm_unit(U_B, lambda cc, b, bk: self.act(Bb[:, cc, 0:nt], b[:, 0:nt], AF.Copy, [bk], [("G2", cc)]))
        fm_unit(U_C, lambda cc, b, bk: self.tt("dve", self.ubuf[:, cc, 2:2 + nt], b[:, 0:nt], xcb[:, cc, 0:nt], ALU.mult,
                                                [bk, ("G1", cc)], [("ubuf", cc)]))
        fm_unit(U_Q, lambda cc, b, bk: self.act(qT(cc)[:, 0:nt], b[:, 0:nt], AF.Copy, [bk], [("GB1", cc)]))
        wtk, wkk = fm_unit(U_K, lambda cc, b, bk: self.act(kT(cc)[:, 0:nt], b[:, 0:nt], AF.Copy, [bk], [("GB1", 4 + cc)],
                                                          scale=128.0 ** -0.5))
        for ci in range(nch):
            off = ci * L
            b, bk = self.bank()
            for kc in range(KC):
                self.mm(b[0:L, :], self.xnT[:, kc, off:off + L], wtk[:, kc, :], kc == 0, kc == KC - 1, [("xnT", kc), wkk], [bk])
            self.act(self.ktm[0:L, ci, :], b[0:L, :], AF.Copy, [bk], [("ktm", ci)], scale=128.0 ** -0.5)
        wtv, wkv = self.wload(l, U_V)
        for ci in range(nch):
            off = ci * L
            b, bk = self.bank()
            for kc in range(KC):
                self.mm(b[0:L, :], self.xnT[:, kc, off:off + L], wtv[:, kc, :], kc == 0, kc == KC - 1, [("xnT", kc), wkv], [bk])
            self.act(self.v1[0:L, ci, :, 0:128], b[0:L, :].rearrange("p (h e) -> p h e", h=4), AF.Copy, [bk, "v1all"], [("v1", ci)])
            for kc in range(KC):
                self.mm(self.psH[0:L, 408:416], self.xnT[:, kc, off:off + L], self.wg[:, l, kc, :], kc == 0, kc == KC - 1,
                        [("xnT", kc), ("wg", l)], [("PS", "H")])
            self.tt("dve", self.gat[0:L, ci, :], self.psH[0:L, 408:416], self.brep[0:L, l, :], ALU.add,
                    [("PS", "H"), ("brep", l)], [("gat", ci)])
        self.S.phase = "mlstm"
        self.mlstm_pre(l, nch, L)
        self.S.phase = "win"
        fm_unit(U_O, lambda cc, b, bk: self.act(self.so[:, cc, 0:nt], b[:, 0:nt], AF.Sigmoid, [bk], [("so", cc)]))
        if stage < 3:
            return
        self.S.phase = "conva"
        for c in range(4):
            acc = self.ybuf[:, c % 2, 0:nt]
            ak = ("ybuf", c % 2)
            uk = [("ubuf", c), ("ubufh",), ("colv", l)]
            self.ts("dve", acc, self.ubuf[:, c, 0:nt], self.cv(l, CV_CONVW + 0 * 4 + c), None, ALU.mult, None, uk, [ak])
            self.stt(acc, self.ubuf[:, c, 1:1 + nt], self.cv(l, CV_CONVW + 1 * 4 + c), acc, ALU.mult, ALU.add, uk + [ak], [ak])
            self.stt(acc, self.ubuf[:, c, 2:2 + nt], self.cv(l, CV_CONVW + 2 * 4 + c), acc, ALU.mult, ALU.add, uk + [ak], [ak])
            self.tt("dve", catT[:, c, 0:nt], acc, Bb[:, c, 0:nt], ALU.mult, [ak, ("G2", c)], [("GB2", c)])
        self.cp("pool", self.uh[l][:, :, :], self.ubuf[:, :, nt:nt + 2], [("ubuf", c) for c in range(4)] + [("ubufh",)], [("uh", l)])
        if stage < 4:
            return
        self.S.phase = "mlstm"
        for ci in range(nch):
            self.mlstm_chunk(l, ci, ci * L, L)
        self.mlstm_tail(l, nch, L)
        if stage < 5:
            return

        def out_units(u0):
            def f(h):
                wt, wk = self.wload(l, u0 + h)
                return [(wt, wk, 0, KC)]
            return f

        self.proj_post(lambda j, s: catT[:, j, s * 128:s * 128 + P], lambda j: ("GB2", j), out_units(U_OUT), nsub, P)
        if stage < 6:
            return
        self.load_grep("norm_mem_post", l)
        self.norm_T(l, xsrc, nt, CV_MEMPRE, self.xnT, "xnT")
        self.S.phase = "attn"
        qmT = self.GB1
        for h2 in range(2):
            wt, wk = self.wload(l, U_MQ + h2)
            for cc in range(4):
                c = h2 * 4 + cc
                b, bk = self.bank()
                for kc in range(KC):
                    self.mm(b[:, 0:nt], wt[:, kc, cc * 128:(cc + 1) * 128], self.xnT[:, kc, 0:nt], kc == 0, kc == KC - 1,
                            [wk, ("xnT", kc)], [bk])
                self.act(qmT[:, c, 0:nt], b[:, 0:nt], AF.Copy, [bk], [("GB1", c)], scale=1.0 / 16.0)
        pT = self.xn[:].rearrange("p a b -> p (a b)").rearrange("p (c t) -> p c t", t=NTP)
        oT = self.GB2
        rden = self.G1
        for h in range(4):
            for mc in range(2):
                b, bk = self.bank()
                for dc in range(2):
                    self.mm(b[:, 0:nt], self.mkT[l][:, h * 2 + dc, mc * 128:(mc + 1) * 128], qmT[:, h * 2 + dc, 0:nt],
                            dc == 0, dc == 1, [("mkT", l), ("GB1", h * 2 + dc)], [bk])
                self.act(pT[:, h * 2 + mc, 0:nt], b[:, 0:nt], AF.Exp, [bk], [("xn", h)])
            b, bk = self.bank()
            for mc in range(2):
                self.mm(b[:, 0:nt], self.onesb[:, :], pT[:, h * 2 + mc, 0:nt], mc == 0, mc == 1,
                        ["onesb", ("xn", h)], [bk])
            self.act(rden[:, h, 0:nt], b[:, 0:nt], AF.Ln, [bk], [("G1", h)])
            self.act(rden[:, h, 0:nt], rden[:, h, 0:nt], AF.Exp, [("G1", h)], [("G1", h)], scale=-1.0)
            for ec in range(2):
                b, bk = self.bank()
                for mc in range(2):
                    self.mm(b[:, 0:nt], self.mv[l][:, mc, h * 256 + ec * 128:h * 256 + (ec + 1) * 128],
                            pT[:, h * 2 + mc, 0:nt], mc == 0, mc == 1, [("mv", l), ("xn", h)], [bk])
                self.tt("dve", oT[:, h * 2 + ec, 0:nt], b[:, 0:nt], rden[:, h, 0:nt], ALU.mult,
                        [bk, ("G1", h)], [("GB2", h * 2 + ec)])
        self.proj_post(lambda j, s: oT[:, j, s * 128:s * 128 + P], lambda j: ("GB2", j), out_units(U_MO), nsub, P)
        if stage < 7:
            return
        self.load_grep("norm_ffn_post", l)
        self.norm_T(l, xsrc, nt, CV_FFNPRE, self.xnT, "xnT")
        self.S.phase = "ffnup"
        pend = None
        for u in range(11):
            wt, wk = self.wload(l, U_UP + u)
            for pj in range(2):
                j = 2 * u + pj
                res = []
                for part in range(2):
                    ch = j if part == 0 else NJ + j
                    b, bk = self.bank()
                    col = part * 256 + pj * 128
                    for kc in range(KC):
                        self.mm(b[:, 0:nt], wt[:, kc, col:col + 128], self.xnT[:, kc, 0:nt], kc == 0, kc == KC - 1,
                                [wk, ("xnT", kc)], [bk])
                    GG, gn = (self.G1, "G1") if j % 2 == 0 else (self.G2, "G2")
                    fa = GG[:, part * 2 + 0, :]
                    acc = GG[:, part * 2 + 1, 0:nt]
                    fk, ak = (gn, part * 2), (gn, part * 2 + 1)
                    fhk = ("ffh", l, ch)
                    self.cp("pool", fa[:, 0:2], self.ffh[l][:, ch, :], [fhk], [fk])
                    self.act(fa[:, 2:2 + nt], b[:, 0:nt], AF.Copy, [bk], [fk])
                    cw = lambda tap, ch=ch: self.cv(l, CV_FCW + tap * 44 + ch)
                    self.act(acc, b[:, 0:nt], AF.Copy, [bk, ("colv", l)], [ak], scale=cw(2))
                    self.stt(acc, fa[:, 1:1 + nt], cw(1), acc, ALU.mult, ALU.add, [fk, ak, ("colv", l)], [ak])
                    self.stt(acc, fa[:, 0:nt], cw(0), acc, ALU.mult, ALU.add, [fk, ak, ("colv", l)], [ak])
                    self.cp("pool", self.ffh[l][:, ch, :], fa[:, nt:nt + 2], [fk], [fhk])
                    res.append((acc, ak))
                if pend is not None:
                    self.act(pend[2], pend[2], AF.Gelu_apprx_tanh, [pend[3]], [pend[3]])
                    self.tt("dve", self.hT[:, pend[4], 0:nt], pend[2], pend[0], ALU.mult, [pend[3], pend[1]], [("hT", pend[4])])
                (acca, aka), (accg, akg) = res
                pend = (acca, aka, accg, akg, j)

        if pend is not None:
            self.act(pend[2], pend[2], AF.Gelu_apprx_tanh, [pend[3]], [pend[3]])
            self.tt("dve", self.hT[:, pend[4], 0:nt], pend[2], pend[0], ALU.mult, [pend[3], pend[1]], [("hT", pend[4])])

        self.proj_down(l, nsub, P)

    def mlstm_pre(self, l, nch, L):
        n4 = 4 * nch
        fl = lambda t: t[0:L, 0:nch, :].rearrange("p c h -> p (c h)")
        self.act(self.lgE[0:L, 0:nch, :], self.gat[0:L, 0:nch, 4:8], AF.Exp, [("gat", ci) for ci in range(nch)], ["lgE"], scale=-1.0)
        self.act(self.lgF[0:L, 0:nch, :], self.lgE[0:L, 0:nch, :], AF.Ln, ["lgE", "ones"], ["lgF"], bias=self.ones[0:L, 0:1])
        b, bk = self.bank()
        self.mm(b[0:L, 0:n4], self.triu[0:L, 0:L], fl(self.lgF), True, True, ["triu", "lgF"], [bk])
        self.mm(b[:, 16:16 + n4], self.ones[0:L, :], fl(self.lgF), True, True, ["ones", "lgF"], [bk])
        self.tt("dve", fl(self.lgA), self.gat[0:L, 0:nch, 0:4], b[0:L, 0:n4].rearrange("p (c h) -> p c h", h=4), ALU.add,
                [("gat", ci) for ci in range(nch)] + [bk], ["lgA"])
        self.cp("dve", fl(self.clS), b[0:L, 0:n4], [bk], ["clS"])
        self.cp("dve", self.clLs[:, 0:nch, :].rearrange("p c h -> p (c h)"), b[:, 16:16 + n4], [bk], ["clLs"])
        b, bk = self.bank()
        self.tr(b[0:n4, 0:L], fl(self.lgA), self.ident[0:L, 0:L], ["lgA", "ident"], [bk])
        self.S.op("dve", lambda e, b=b: e.tensor_reduce(out=self.amx[0:n4, 0:1], in_=b[0:n4, 0:L], axis=AX.X, op=ALU.max),
                  [bk], ["amx"])
        self.ts("dve", self.amd[0:n4, 0:n4], self.ident[0:n4, 0:n4], self.amx[0:n4, 0:1], None, ALU.mult, None,
                ["ident", "amx"], ["amd"])
        b2, bk2 = self.bank()
        self.mm(b2[:, 0:n4], self.ones[0:n4, :], self.amd[0:n4, 0:n4], True, True, ["ones", "amd"], [bk2])
        self.cp("dve", self.amaxS[:, 0:nch, :].rearrange("p c h -> p (c h)"), b2[:, 0:n4], [bk2],
                [("amaxS", ci) for ci in range(nch)])

    def mlstm_chunk(self, l, ci, off, L):
        ms = self.ms[:, ci, :]
        K = lambda n: ("ms", ci, n)
        qT = lambda h: self.GB1[:, h, off:off + L]
        kT = lambda h: self.GB1[:, 4 + h, off:off + L]
        hb = [(self.psA, ("PS", "A")), (self.psS, ("PS", "S")), (self.psH, ("PS", "H")), (self.psC, ("PS", "C"))]
        mr = self.mrep[l]
        self.tt("dve", ms[:, 0:4], mr[:, :], self.amaxS[:, ci, :], ALU.max, [("mrep", l), ("amaxS", ci)], [K("mu")])
        self.tt("dve", ms[:, 4:8], mr[:, :], ms[:, 0:4], ALU.subtract, [("mrep", l), K("mu")], [K("dm")])
        self.tt("dve", ms[0:L, 8:12], self.lgA[0:L, ci, :], ms[0:L, 0:4], ALU.subtract, ["lgA", K("mu")], [K("ea")])
        self.tt("dve", ms[0:L, 12:16], self.clS[0:L, ci, :], ms[0:L, 0:4], ALU.subtract, ["clS", K("mu")], [K("ea")])
        self.act(ms[:, 16:20], ms[:, 4:8], AF.Exp, [K("dm")], [K("wc")])
        self.act(ms[0:L, 20:28], ms[0:L, 8:16], AF.Exp, [K("ea")], [K("e")])
        self.tt("dve", mr[:, :], ms[:, 0:4], self.clLs[:, ci, :], ALU.subtract, [K("mu"), "clLs"], [("mrep", l)])
        for h in range(4):
            e_h = ms[0:L, 20 + h:21 + h]
            wc_h = ms[:, 16 + h:17 + h]
            Ck = ("Cn", l, h)
            B, Bk = hb[h]
            self.act(self.Csb[:, h, 0:129], self.Cn[l][:, h, :], AF.Copy, [Ck, K("wc")], [("Csb", h)], scale=wc_h)
            self.ts("dve", self.kw[0:L, h, :], self.ktm[0:L, ci, h * 128:(h + 1) * 128], e_h, None, ALU.mult, None,
                    [("ktm", ci), K("e")], [("kw", h)])
            self.mm(B[0:L, 0:L], kT(h), qT(h), True, True, [("GB1", 4 + h), ("GB1", h)], [Bk])
            self.stt(self.smb[0:L, h, 0:L], B[0:L, 0:L], e_h, self.triu[0:L, 0:L], ALU.mult, ALU.mult,
                     [Bk, K("e"), "triu"], [("smb", h)])
            H = B[0:L, 128:257]
            v1h = self.v1[0:L, ci, h, 0:129]
            self.mm(H, self.smb[0:L, h, 0:L], v1h, True, False, [("smb", h), ("v1", ci), "v1all"], [Bk])
            self.mm(H, qT(h), self.Csb[:, h, 0:129], False, True, [("GB1", h), ("Csb", h)], [Bk])
            Cps = B[:, 257:386]
            self.mm(Cps, self.kw[0:L, h, :], v1h, True, True, [("kw", h), ("v1", ci), "v1all"], [Bk])
            self.stt(self.Cn[l][:, h, :], self.Cn[l][:, h, :], wc_h, Cps, ALU.mult, ALU.add, [Ck, K("wc"), Bk], [Ck])
            self.act(self.ubuf[0:L, ci, h * 129:(h + 1) * 129], H, AF.Copy, [Bk], [("ubuf", ci)])

    def mlstm_tail(self, l, nch, L):
        nm = self.nms
        ukeys = [("ubuf", c) for c in range(nch)]
        Hall = self.ubuf[0:L, 0:nch, :].rearrange("p c (h e) -> p c h e", e=129)
        fl = lambda a, b: nm[0:L, 0:nch, a:b]
        den = self.ubuf[0:L, 0:nch, 128:516:129] if False else Hall[:, :, :, 128]
        lower = self.ms[0:L, 0:nch, 24:28]
        mk = [("ms", c, "e") for c in range(nch)]
        for c in range(nch):
            self.stt(nm[0:L, c, 0:4], Hall[:, c, :, 128], -1.0, self.ms[0:L, c, 24:28], ALU.mult, ALU.max,
                     [("ubuf", c), ("ms", c, "e")], [("nms", c, "t1")])
            self.tt("dve", nm[0:L, c, 4:8], Hall[:, c, :, 128], nm[0:L, c, 0:4], ALU.max, [("ubuf", c), ("nms", c, "t1")], [("nms", c, "da")])
            for h in range(4):
                self.act(self.junk[0:L, h * 128:(h + 1) * 128], Hall[:, c, h, 0:128], AF.Square, [("ubuf", c)],
                         [("junkq", h), ("nms", c, ("ssq", h))], accum_out=nm[0:L, c, 12 + h:13 + h])
        da_k = [("nms", c, "da") for c in range(nch)]
        sq_k = [("nms", c, ("ssq", h)) for c in range(nch) for h in range(4)]
        self.recip(fl(8, 12), fl(4, 8), da_k, ["nms_rd"])
        self.tt("dve", fl(16, 20), fl(8, 12), fl(8, 12), ALU.mult, ["nms_rd"], ["nms_r2"])
        self.tt("dve", fl(20, 24), fl(12, 16), fl(16, 20), ALU.mult, sq_k + ["nms_r2"], ["nms_v"])
        self.ts("dve", fl(20, 24), fl(20, 24), 1.0 / 128.0, EPS, ALU.mult, ALU.add, ["nms_v"], ["nms_v2"])
        for c in range(nch):
            self.tt("pool", nm[0:L, c, 24:28], nm[0:L, c, 20:24], self.nhalf[0:L, 0:4], ALU.pow, ["nms_v2", "nhalf"], [("nms", c, "rs")])
        self.tt("dve", fl(28, 32), fl(8, 12), fl(24, 28), ALU.mult, ["nms_rd"] + [("nms", c, "rs") for c in range(nch)], ["nms_sc"])
        for h in range(4):
            self.ts("dve", self.so[:, h, 0:nch * L], self.so[:, h, 0:nch * L], self.cv(l, CV_NW + h), None, ALU.mult, None,
                    [("so", h), ("colv", l)], [("so", h)])
        for c in range(nch):
            off = c * L
            for h in range(4):
                self.act(self.hn[0:L, h, :], Hall[:, c, h, 0:128], AF.Copy, [("ubuf", c), "nms_sc"], [("hn", h)],
                         scale=nm[0:L, c, 28 + h:29 + h])
            b, bk = self.bank()
            for h in range(4):
                self.tr(b[:, h * L:(h + 1) * L], self.hn[0:L, h, :], self.ident[0:L, 0:L], [("hn", h), "ident"], [bk])
            self.tt("dve", self.GB2[:, 4:8, off:off + L], b[:, 0:4 * L].rearrange("p (h t) -> p h t", h=4),
                    self.so[:, :, off:off + L], ALU.mult, [bk] + [("so", h) for h in range(4)], [("GB2", 4 + h) for h in range(4)])

    def init_sample(self, r):
        d = self.d
        for l in range(DEPTH):
            self.dma("sp", self.Cn[l][:, :, 0:128], d["sC"][l, r].rearrange("h d e -> d h e"), ("ldC", l), [],
                     [("Cn", l, h) for h in range(4)])
            self.dma("sp", self.stA[0:8, :], d["sconv"][l, r].rearrange("j (c p) -> (j c) p", p=128), ("stA", 0), [], ["stA"])
            self.dma("sp", self.stA[8:12, :], d["sn"][l, r], ("stA", 1), [], ["stA"])
            self.dma("sp", self.stB[0:88, :], d["sffn"][l, r].rearrange("j (c p) -> (j c) p", p=128), ("stB", 0), [], ["stB"])
            b, bk = self.bank()
            self.tr(b[:, 0:128], self.stA[:], self.ident[:], ["stA", "ident"], [bk])
            self.tr(b[:, 128:256], self.stB[:], self.ident[:], ["stB", "ident"], [bk])
            self.cp("dve", self.uh[l][:, :, :], b[:, 0:8].rearrange("p (j c) -> p c j", j=2), [bk], [("uh", l)])
            self.cp("dve", self.Cn[l][:, :, 128:129], b[:, 8:12].rearrange("p (h o) -> p h o", o=1), [bk],
                    [("Cn", l, h) for h in range(4)])
            self.cp("dve", self.ffh[l][:, :, :], b[:, 128:216].rearrange("p (j c) -> p c j", j=2), [bk],
                    [("ffh", l, ch) for ch in range(44)])
            self.dma("sp", self.mrep[l][:, :], d["sm"][l, r].partition_broadcast(128), ("ldm", l), [], [("mrep", l)])
            self.dma("pool", self.xn[:, 0:2, :], d["cmk"][l, r].rearrange("(mc p) d -> p mc d", p=128), "ldmk", [],
                     [("xn", 0), ("xn", 1)])
            for g in range(2):
                for cc in range(4):
                    for mc in range(2):
                        c = g * 4 + cc
                        self.tr(self.pstr[:, (cc * 2 + mc) * 128:(cc * 2 + mc + 1) * 128], self.xn[:, mc, c * 128:(c + 1) * 128],
                                self.identb[:, :], [("xn", mc), "identb"], [("PS", "tr")])
                self.cp("dve", self.mkT[l][:, g * 4:(g + 1) * 4, :], self.pstr[:, :].rearrange("p (c m) -> p c m", c=4),
                        [("PS", "tr")], [("mkT", l)])
            self.dma("pool", self.mv[l][:, :, :], d["cmv"][l, r].rearrange("(mc p) d -> p mc d", p=128), ("ldmv", l), [],
                     [("mv", l)])

    def init_prompt(self):
        d = self.d
        for l in range(DEPTH):
            self.memset("dve", self.Cn[l][:], 0.0, [("Cn", l, h) for h in range(4)])
            self.memset("dve", self.mrep[l][:], 0.0, [("mrep", l)])
            self.memset("dve", self.uh[l][:], 0.0, [("uh", l)])
            self.memset("dve", self.ffh[l][:], 0.0, [("ffh", l, ch) for ch in range(44)])
            self.dma("sp", self.ybuf[:, 0:2, :], d["mem"].rearrange("(mc p) d -> p mc d", p=128), "ldmem", [],
                     [("ybuf", 0), ("ybuf", 1)])
            self.norm_T(l, [(self.ybuf[:, mc, :], ("ybuf", mc), 128) for mc in range(2)], NMEM, CV_MEMKV, self.xnT, "xnT")
            for h2 in range(2):
                wt, wk = self.wload(l, U_MK + h2)
                for cc in range(4):
                    b, bk = self.bank()
                    for kc in range(KC):
                        self.mm(b[:, 0:NMEM], wt[:, kc, cc * 128:(cc + 1) * 128], self.xnT[:, kc, 0:NMEM], kc == 0, kc == KC - 1,
                                [wk, ("xnT", kc)], [bk])
                    self.act(self.mkT[l][:, h2 * 4 + cc, :], b[:, 0:NMEM], AF.Copy, [bk], [("mkT", l)])
                for mc in range(2):
                    b, bk = self.bank()
                    for kc in range(KC):
                        self.mm(b[:, :], self.xnT[:, kc, mc * 128:(mc + 1) * 128], wt[:, kc, :], kc == 0, kc == KC - 1,
                                [wk, ("xnT", kc)], [bk])
                    self.act(self.G2[:, mc, 0:512], b[:, :], AF.Copy, [bk], [("G2", mc)])
                    self.dma("sp", d["o_mk_p"][l, mc * 128:(mc + 1) * 128, h2 * 512:(h2 + 1) * 512], self.G2[:, mc, 0:512],
                             ("omk", mc), [("G2", mc)], [])
            for h2 in range(2):
                wt, wk = self.wload(l, U_MV + h2)
                for mc in range(2):
                    b, bk = self.bank()
                    for kc in range(KC):
                        self.mm(b[:, :], self.xnT[:, kc, mc * 128:(mc + 1) * 128], wt[:, kc, :], kc == 0, kc == KC - 1,
                                [wk, ("xnT", kc)], [bk])
                    self.act(self.G2[:, 2 + mc, 0:512], b[:, :], AF.Copy, [bk], [("G2", 2 + mc)])
                    self.cp("dve", self.mv[l][:, mc, h2 * 512:(h2 + 1) * 512], b[:, :], [bk], [("mv", l)])
                    self.dma("sp", d["o_mv_p"][l, mc * 128:(mc + 1) * 128, h2 * 512:(h2 + 1) * 512], self.G2[:, 2 + mc, 0:512],
                             ("omv", mc), [("G2", 2 + mc)], [])

    def finalize(self, kind, r):
        d = self.d
        sfx = "_s" if kind == "s" else "_p"
        for l in range(DEPTH):
            Ck = [("Cn", l, h) for h in range(4)]
            self.dma("sp", d["o_C" + sfx][l, r].rearrange("h d e -> d h e"), self.Cn[l][:, :, 0:128], ("oC", l), Ck, [])
            self.cp("dve", self.tail[:, 0:8].rearrange("p (j c) -> p c j", j=2), self.uh[l][:, :, :], [("uh", l)], ["tail"])
            self.cp("dve", self.tail[:, 8:12].rearrange("p (h o) -> p h o", o=1), self.Cn[l][:, :, 128:129], Ck, ["tail"])
            self.cp("dve", self.tail[:, 32:120].rearrange("p (j c) -> p c j", j=2), self.ffh[l][:, :, :],
                    [("ffh", l, ch) for ch in range(44)], ["tail"])
            b, bk = self.bank()
            self.tr(b[0:12, 0:128], self.tail[:, 0:12], self.ident[:, :], ["tail", "ident"], [bk])
            self.tr(b[0:88, 128:256], self.tail[:, 32:120], self.ident[:, :], ["tail", "ident"], [bk])
            self.cp("dve", self.tailr[0:12, :], b[0:12, 0:128], [bk], ["tailr"])
            self.cp("dve", self.tailr2[0:88, :], b[0:88, 128:256], [bk], ["tailr2"])
            self.dma("sp", d["o_conv" + sfx][l, r].rearrange("j (c p) -> (j c) p", p=128), self.tailr[0:8, :], "otail", ["tailr"], [])
            self.dma("sp", d["o_n" + sfx][l, r], self.tailr[8:12, :], "otail", ["tailr"], [])
            self.dma("sp", d["o_ffn" + sfx][l, r].rearrange("j (c p) -> (j c) p", p=128), self.tailr2[0:88, :], "otail2", ["tailr2"], [])
            self.dma("sp", d["o_m" + sfx][l, r:r + 1, :], self.mrep[l][0:1, :], ("om", l), [("mrep", l)], [])

    def build(self):
        d_ = self.declare()
        self.alloc()
        self.consts()
        for l in range(DEPTH):
            self.convert(l)
        d = self.d
        for r in range(2):
            self.init_sample(r)
            self.dma("sp", self.x[0:64, 0, :], d["xs"][r], "ldx", [], [("x", 0)])
            for l in range(DEPTH):
                self.layer(l, 64, 64)
            self.dma("sp", d["ys"][r], self.x[0:64, 0, :], "sty", [("x", 0)], [])
            self.finalize("s", r)
        if self.TP:
            self.init_prompt()
            ns = NTP // 128
            for ti in range(self.TP // NTP):
                src = d["xp"][ti * NTP:(ti + 1) * NTP, :].rearrange("(s p) d -> p s d", p=128)
                dst = d["yp"][ti * NTP:(ti + 1) * NTP, :].rearrange("(s p) d -> p s d", p=128)
                for s in range(ns):
                    self.dma("sp", self.x[:, s, :], src[:, s, :], ("ldx", s), [], [("x", s)])
                for l in range(DEPTH):
                    self.layer(l, NTP, 128)
                for s in range(ns):
                    self.dma("sp", dst[:, s, :], self.x[:, s, :], ("sty", s), [("x", s)], [])
            self.finalize("p", 0)
        self.S.emit(self.nc)
        self.st.close()
        return self.nc


_WNAMES = ("norm_mix_pre", "w_in", "b_gates", "conv_w", "mlstm_norm_w", "w_out", "norm_mix_post",
           "norm_mem_pre", "norm_mem_kv", "w_mq", "w_mk", "w_mv", "w_mo", "norm_mem_post",
           "norm_ffn_pre", "w_up", "ffn_conv_w", "w_down", "norm_ffn_post")

TP_PER_CORE = 16384
PROMPT_CORES = (0, 4)


def kernel(**inputs):
    f = lambda a: np.ascontiguousarray(np.asarray(a, dtype=np.float32))
    inp = {k: f(v) for k, v in inputs.items()}
    TP = TP_PER_CORE
    nc = Builder(TP).build()
    in_maps = []
    for c in range(NCORES):
        sl = slice(2 * c, 2 * c + 2)
        m = {
            "xs": inp["x_sample"][sl],
            "cmk": f(inp["cache_mem_k"][:, sl].reshape(DEPTH, 2, NMEM, D)),
            "cmv": f(inp["cache_mem_v"][:, sl].reshape(DEPTH, 2, NMEM, D)),
            "sconv": f(inp["state_conv"][:, sl]),
            "sC": f(inp["state_mlstm_C"][:, sl]),
            "sn": f(inp["state_mlstm_n"][:, sl]),
            "sm": f(inp["state_mlstm_m"][:, sl]),
            "sffn": f(inp["state_ffn_conv"][:, sl]),
        }
        if TP and c in PROMPT_CORES:
            pb = PROMPT_CORES.index(c)
            m["mem"] = inp["mem_prompt"][pb]
            m["xp"] = f(inp["x_prompt"][pb, 0:TP])
        else:
            m["mem"] = np.zeros((NMEM, D), np.float32)
            m["xp"] = np.zeros((max(TP, NTP), D), np.float32)
        for n in _WNAMES:
            m[n] = inp[n]
        in_maps.append(m)
    res = run_bass_kernel_spmd(nc, in_maps, core_ids=list(range(NCORES))).results
    B = inp["x_prompt"].shape[0]
    y_p = np.zeros((B, PSEQ, D), np.float32)
    mk_p = np.zeros((DEPTH, B, NMEM, 4, 256), np.float32)
    mv_p = np.zeros_like(mk_p)
    conv_p = np.zeros((DEPTH, B, 2, 512), np.float32)
    C_p = np.zeros((DEPTH, B, 4, 128, 128), np.float32)
    n_p = np.zeros((DEPTH, B, 4, 128), np.float32)
    m_p = np.zeros((DEPTH, B, 4), np.float32)
    ff_p = np.zeros((DEPTH, B, 2, 2 * DFF), np.float32)
    if TP:
        for b in range(B):
            r = res[PROMPT_CORES[b]]
            y_p[b, 0:TP] = r["yp"][0:TP]
            mk_p[:, b] = r["o_mk_p"].reshape(DEPTH, NMEM, 4, 256)
            mv_p[:, b] = r["o_mv_p"].reshape(DEPTH, NMEM, 4, 256)
            conv_p[:, b] = r["o_conv_p"][:, 0]
            C_p[:, b] = r["o_C_p"][:, 0]
            n_p[:, b] = r["o_n_p"][:, 0]
            m_p[:, b] = r["o_m_p"][:, 0]
            ff_p[:, b] = r["o_ffn_p"][:, 0]
    cat = lambda k, ax: np.concatenate([res[c][k] for c in range(NCORES)], axis=ax)
    y_s = cat("ys", 0)
    conv_s = cat("o_conv_s", 1)
    C_s = cat("o_C_s", 1)
    n_s = cat("o_n_s", 1)
    m_s = cat("o_m_s", 1)
    ff_s = cat("o_ffn_s", 1)
    return (y_p, y_s, mk_p, mv_p, conv_p, conv_s, C_p, C_s, n_p, n_s, m_p, m_s, ff_p, ff_s)
```
